# Optimizing a Trainium2 kernel written in Bass

```python
import math
import jax, jax.numpy as jnp
from jax import lax
import numpy as np

D_MODEL = 2048
BATCH = 4
SEQ = 2048
DEPTH = 1

N_MEM = 256
HEAD_DIM = D_MODEL // 16
BRANCH_W = D_MODEL // 2
N_BRANCH = 3
DIFF_HEADS = 8
DIFF_QK = HEAD_DIM // 2
DIFF_V = HEAD_DIM
WIN_HEADS = 8
WIN_KV_HEADS = 2
WIN_DIM = HEAD_DIM
WINDOW = 128
BLOCK = 128
MEM_HEADS = 4
MEM_DIM = BRANCH_W // MEM_HEADS
SEG_DIFF_Q = DIFF_HEADS * 2 * DIFF_QK
SEG_DIFF_K = DIFF_HEADS * 2 * DIFF_QK
SEG_DIFF_V = DIFF_HEADS * DIFF_V
SEG_WIN_Q = WIN_HEADS * WIN_DIM
SEG_WIN_K = WIN_KV_HEADS * WIN_DIM
SEG_WIN_V = WIN_KV_HEADS * WIN_DIM
SEG_MEM_Q = MEM_HEADS * MEM_DIM
SEGMENTS = (SEG_DIFF_Q, SEG_DIFF_K, SEG_DIFF_V, SEG_WIN_Q, SEG_WIN_K, SEG_WIN_V, SEG_MEM_Q)
IN_W = SEG_DIFF_Q + SEG_DIFF_K + SEG_DIFF_V + SEG_WIN_Q + SEG_WIN_K + SEG_WIN_V + SEG_MEM_Q
D_FF = 5504
CONV_W = 3
REL_BUCKETS = 32
REL_MAX_DIST = 128
REL_HEADS = DIFF_HEADS + WIN_HEADS
ALPHA = (2 * DEPTH) ** 0.25
BETA = (8 * DEPTH) ** -0.25
LN_EPS = 1e-5
NEG = -1e30

kernel_name = "hybrid_diffattn_wingqa_mem_convffn_deepnorm"


def layer_norm(x, g, b):
    xf = x.astype(jnp.float32)
    mu = jnp.mean(xf, axis=-1, keepdims=True)
    var = jnp.mean(jnp.square(xf - mu), axis=-1, keepdims=True)
    return ((xf - mu) * lax.rsqrt(var + LN_EPS) * g.astype(jnp.float32) + b.astype(jnp.float32)).astype(x.dtype)


def rms_norm(x, g):
    xf = x.astype(jnp.float32)
    ms = jnp.mean(jnp.square(xf), axis=-1, keepdims=True)
    return (xf * lax.rsqrt(ms + LN_EPS) * g.astype(jnp.float32)).astype(x.dtype)


def t5_bucket(rel):
    half = REL_BUCKETS // 2
    max_exact = half // 2
    ret = jnp.where(rel > 0, half, 0)
    n = jnp.abs(rel)
    nf = jnp.maximum(n, 1).astype(jnp.float32)
    large = max_exact + (jnp.log(nf / max_exact) / math.log(REL_MAX_DIST / max_exact)
                         * (half - max_exact)).astype(jnp.int32)
    large = jnp.minimum(large, half - 1)
    return ret + jnp.where(n < max_exact, n, large)


def diff_attention(q, k, v, lq1, lk1, lq2, lk2, subln_g, rel_table, lambda_init):
    B, S = q.shape[0], q.shape[1]
    nblk = S // BLOCK
    lam = (jnp.exp(jnp.sum(lq1.astype(jnp.float32) * lk1.astype(jnp.float32)))
           - jnp.exp(jnp.sum(lq2.astype(jnp.float32) * lk2.astype(jnp.float32))) + lambda_init)
    scale = DIFF_QK ** -0.5
    table = rel_table[:, :DIFF_HEADS].astype(jnp.float32)
    kpos = jnp.arange(S)

    def one_block(i):
        qb = lax.dynamic_slice_in_dim(q, i * BLOCK, BLOCK, axis=1)
        qpos = i * BLOCK + jnp.arange(BLOCK)
        bias = jnp.transpose(table[t5_bucket(kpos[None, :] - qpos[:, None])], (2, 0, 1))
        s = jnp.einsum('bqhcd,bkhcd->bhcqk', qb, k).astype(jnp.float32) * scale + bias[None, :, None]
        p = jax.nn.softmax(s, axis=-1)
        a = p[:, :, 0] - lam * p[:, :, 1]
        return jnp.einsum('bhqk,bkhd->bqhd', a.astype(v.dtype), v)

    o = lax.map(one_block, jnp.arange(nblk))
    o = jnp.transpose(o, (1, 0, 2, 3, 4)).reshape(B, S, DIFF_HEADS, DIFF_V)
    o = rms_norm(o, subln_g) * (1.0 - lambda_init)
    return o.reshape(B, S, DIFF_HEADS * DIFF_V)


def windowed_gqa(q, k, v, sink, rel_table):
    B, S = q.shape[0], q.shape[1]
    nblk = S // BLOCK
    G = WIN_HEADS // WIN_KV_HEADS
    scale = WIN_DIM ** -0.5
    qb = q.reshape(B, nblk, BLOCK, WIN_KV_HEADS, G, WIN_DIM)

    def band(t):
        tp = jnp.pad(t, ((0, 0), (WINDOW, WINDOW), (0, 0), (0, 0)))
        tb = tp.reshape(B, nblk + 2, BLOCK, WIN_KV_HEADS, WIN_DIM)
        return jnp.concatenate([tb[:, :-2], tb[:, 1:-1], tb[:, 2:]], axis=2)

    kb, vb = band(k), band(v)
    blk = jnp.arange(nblk)[:, None] * BLOCK
    qpos = blk + jnp.arange(BLOCK)[None, :]
    kpos = blk - WINDOW + jnp.arange(3 * BLOCK)[None, :]
    rel = kpos[:, None, :] - qpos[:, :, None]
    valid = (jnp.abs(rel) <= WINDOW) & (kpos[:, None, :] >= 0) & (kpos[:, None, :] < S)
    bias = rel_table[:, DIFF_HEADS:].astype(jnp.float32)[t5_bucket(rel)]
    bias = jnp.transpose(bias.reshape(nblk, BLOCK, 3 * BLOCK, WIN_KV_HEADS, G), (3, 4, 0, 1, 2))
    s = jnp.einsum('bnqhgd,bnkhd->bhgnqk', qb, kb).astype(jnp.float32) * scale + bias[None]
    s = jnp.where(valid, s, NEG)
    sk = sink.astype(jnp.float32).reshape(WIN_KV_HEADS, G)[None, :, :, None, None, None]
    m = jnp.maximum(jnp.max(s, axis=-1, keepdims=True), sk)
    p = jnp.exp(s - m)
    p = p / (jnp.sum(p, axis=-1, keepdims=True) + jnp.exp(sk - m))
    o = jnp.einsum('bhgnqk,bnkhd->bnqhgd', p.astype(v.dtype), vb)
    return o.reshape(B, S, WIN_HEADS * WIN_DIM)


def memory_attention(q, mk, mv):
    B, S = q.shape[0], q.shape[1]
    s = jnp.einsum('bqhd,bkhd->bhqk', q, mk).astype(jnp.float32) * (MEM_DIM ** -0.5)
    p = jax.nn.softmax(s, axis=-1)
    o = jnp.einsum('bhqk,bkhd->bqhd', p.astype(mv.dtype), mv)
    return o.reshape(B, S, MEM_HEADS * MEM_DIM)


def conv_ffn(h, w_up, conv_w, conv_b, w_down):
    u = h @ w_up
    up = jnp.pad(u, ((0, 0), (1, 1), (0, 0)))
    u = up[:, :-2] * conv_w[0] + up[:, 1:-1] * conv_w[1] + up[:, 2:] * conv_w[2] + conv_b
    val, gate = jnp.split(u, 2, axis=-1)
    return (jax.nn.gelu(gate) * val) @ w_down


def setup_inputs(seed: int = 0) -> dict:
    key = jax.random.key(seed)
    ks = jax.random.split(key, 26)
    f32 = jnp.float32
    nrm = lambda k, shape: jax.random.normal(k, shape, f32)
    D = D_MODEL
    col_scale = jnp.concatenate([
        jnp.full((SEG_DIFF_Q + SEG_DIFF_K,), 1.0, f32),
        jnp.full((SEG_DIFF_V,), BETA, f32),
        jnp.full((SEG_WIN_Q + SEG_WIN_K,), 1.0, f32),
        jnp.full((SEG_WIN_V,), BETA, f32),
        jnp.full((SEG_MEM_Q,), 1.0, f32)]) * (D ** -0.5)
    mem_scale = jnp.concatenate([jnp.full((BRANCH_W,), 1.0, f32), jnp.full((BRANCH_W,), BETA, f32)]) * (D ** -0.5)
    return {
        "x": nrm(ks[0], (BATCH, SEQ, D)),
        "mem": nrm(ks[1], (BATCH, N_MEM, D)),
        "ln_in_g": 1.0 + 0.02 * nrm(ks[2], (D,)),
        "ln_in_b": 0.02 * nrm(ks[3], (D,)),
        "rel_table": 0.2 * nrm(ks[4], (REL_BUCKETS, REL_HEADS)),
        "w_in": nrm(ks[5], (DEPTH, D, IN_W)) * col_scale,
        "w_mem_kv": nrm(ks[6], (DEPTH, D, 2 * BRANCH_W)) * mem_scale,
        "diff_lq1": 0.1 * nrm(ks[7], (DEPTH, DIFF_QK)),
        "diff_lk1": 0.1 * nrm(ks[8], (DEPTH, DIFF_QK)),
        "diff_lq2": 0.1 * nrm(ks[9], (DEPTH, DIFF_QK)),
        "diff_lk2": 0.1 * nrm(ks[10], (DEPTH, DIFF_QK)),
        "diff_subln_g": 1.0 + 0.02 * nrm(ks[11], (DEPTH, DIFF_V)),
        "win_sink": 0.5 * nrm(ks[12], (DEPTH, WIN_HEADS)),
        "w_gate": nrm(ks[13], (DEPTH, D, N_BRANCH * D)) * (D ** -0.5),
        "b_gate": 0.02 * nrm(ks[14], (DEPTH, N_BRANCH * D)),
        "w_branch": nrm(ks[15], (DEPTH, N_BRANCH, BRANCH_W, D)) * (BRANCH_W ** -0.5),
        "w_o": nrm(ks[16], (DEPTH, D, D)) * (BETA * D ** -0.5),
        "ln1_g": 1.0 + 0.02 * nrm(ks[17], (DEPTH, D)),
        "ln1_b": 0.02 * nrm(ks[18], (DEPTH, D)),
        "w_up": nrm(ks[19], (DEPTH, D, 2 * D_FF)) * (D ** -0.5),
        "conv_w": nrm(ks[20], (DEPTH, CONV_W, 2 * D_FF)) * (CONV_W ** -0.5),
        "conv_b": 0.02 * nrm(ks[21], (DEPTH, 2 * D_FF)),
        "w_down": nrm(ks[22], (DEPTH, D_FF, D)) * (BETA * D_FF ** -0.5),
        "ln2_g": 1.0 + 0.02 * nrm(ks[23], (DEPTH, D)),
        "ln2_b": 0.02 * nrm(ks[24], (DEPTH, D)),
    }


def reference(x, mem, ln_in_g, ln_in_b, rel_table, w_in, w_mem_kv, diff_lq1, diff_lk1, diff_lq2,
              diff_lk2, diff_subln_g, win_sink, w_gate, b_gate, w_branch, w_o, ln1_g, ln1_b,
              w_up, conv_w, conv_b, w_down, ln2_g, ln2_b):
    B, S, _ = x.shape
    NM = mem.shape[1]
    split_points = np.cumsum(SEGMENTS)[:-1].tolist()
    h = layer_norm(x, ln_in_g, ln_in_b)
    for l in range(DEPTH):
        lambda_init = 0.8 - 0.6 * math.exp(-0.3 * l)
        proj = h @ w_in[l]
        dq, dk, dv, wq, wk, wv, mq = jnp.split(proj, split_points, axis=-1)
        a = diff_attention(dq.reshape(B, S, DIFF_HEADS, 2, DIFF_QK),
                           dk.reshape(B, S, DIFF_HEADS, 2, DIFF_QK),
                           dv.reshape(B, S, DIFF_HEADS, DIFF_V),
                           diff_lq1[l], diff_lk1[l], diff_lq2[l], diff_lk2[l], diff_subln_g[l],
                           rel_table, lambda_init)
        b = windowed_gqa(wq.reshape(B, S, WIN_HEADS, WIN_DIM),
                         wk.reshape(B, S, WIN_KV_HEADS, WIN_DIM),
                         wv.reshape(B, S, WIN_KV_HEADS, WIN_DIM),
                         win_sink[l], rel_table)
        mk, mv = jnp.split(mem @ w_mem_kv[l], 2, axis=-1)
        c = memory_attention(mq.reshape(B, S, MEM_HEADS, MEM_DIM),
                             mk.reshape(B, NM, MEM_HEADS, MEM_DIM),
                             mv.reshape(B, NM, MEM_HEADS, MEM_DIM))
        branches = jnp.stack([a, b, c], axis=2)
        widened = jnp.einsum('bsnc,ncd->bsnd', branches, w_branch[l])
        gates = jax.nn.sigmoid(h @ w_gate[l] + b_gate[l]).reshape(B, S, N_BRANCH, D_MODEL)
        mix = jnp.sum(gates * widened, axis=2) @ w_o[l]
        h = layer_norm(ALPHA * h + mix, ln1_g[l], ln1_b[l])
        ffn = conv_ffn(h, w_up[l], conv_w[l], conv_b[l], w_down[l])
        h = layer_norm(ALPHA * h + ffn, ln2_g[l], ln2_b[l])
    return h
```

```python
import math
from contextlib import ExitStack
import numpy as np
import concourse.bass as bass
import concourse.mybir as mybir
from concourse.bass_utils import run_bass_kernel_spmd

F32 = mybir.dt.float32
BF16 = mybir.dt.bfloat16
AF = mybir.ActivationFunctionType
ALU = mybir.AluOpType
AX = mybir.AxisListType

D = 2048
KC = 16
S = 2048
T = 1024
NQ = 1026
OTH = 1028
HTW = 2052
IN_W = 5632
D_FF = 5504
NJ = 43
ALPHA = 2 ** 0.25
LN_EPS = 1e-5
LAMBDA_INIT = 0.2
QCH = [(0, 342), (342, 342), (684, 342)]
WCOLS = 256
NSLOT = 4
MASKV = -200.0

PP_LN = 0
PP_BG = 96
PP_CW = 144
PP_CB = 402
PP_SG = 488
PP_LQ = 489
PP_SINK = 745
PP_CHI = 753
PP_CLO = 761
PP_COT = 769
PP_FLAG = 777
PP_N = 780


import os
DBGCUT = int(os.environ.get('DBGCUT', '0'))


class _Cut(Exception):
    pass


class Sched:
    SEM_LIMIT = 20000

    def __init__(self):
        self.ops = []
        self.last_writer = {}
        self.readers = {}
        self.floor = []
        self.dma_count = {}

    def op(self, eng, fn, reads=(), writes=(), dma=None, nobar=False):
        i = len(self.ops)
        deps = set()
        for k in reads:
            w = self.last_writer.get(k)
            if w is not None:
                deps.add(w)
            if k.startswith("ps"):
                for r in self.readers.get(k, ()):
                    if self.ops[r]["eng"] != eng:
                        deps.add(r)
        for k in writes:
            w = self.last_writer.get(k)
            if w is not None:
                deps.add(w)
            for r in self.readers.get(k, ()):
                deps.add(r)
        if not nobar:
            deps.update(self.floor)
        deps.discard(i)
        o = dict(eng=eng, fn=fn, deps=deps, sig=False, dma=dma)
        if dma is not None:
            semkey, n = dma
            c = self.dma_count.get(semkey, 0) + n
            self.dma_count[semkey] = c
            o["dval"] = 16 * c
        self.ops.append(o)
        for k in reads:
            self.readers.setdefault(k, []).append(i)
        for k in writes:
            self.last_writer[k] = i
            self.readers[k] = []
        return i

    def barrier(self):
        self.marks = getattr(self, "marks", [])
        self.marks.append(len(self.ops))
        last = {}
        for i, o in enumerate(self.ops):
            last[o["eng"]] = i
            if o["dma"] is not None:
                last[("dma", o["dma"][0])] = i
        self.floor = list(last.values())

    def emit(self, nc, stack, block):
        ops = self.ops
        for o in ops:
            for d in o["deps"]:
                po = ops[d]
                if po["dma"] is None:
                    if po["eng"] == "pe" and o["eng"] == "pe":
                        continue
                    po["sig"] = True
        cnt = {}
        for o in ops:
            if o["dma"] is None and o["sig"]:
                e = o["eng"]
                c = cnt.get(e, 0)
                o["sv"] = (e, c // self.SEM_LIMIT, c % self.SEM_LIMIT + 1)
                cnt[e] = c + 1
        self.mark_pe = []
        for mk in getattr(self, "marks", []):
            self.mark_pe.append(sum(1 for o in ops[:mk] if o["dma"] is None and o["sig"] and o["eng"] == "pe"))
        sems = {}

        def getsem(key):
            if key not in sems:
                sems[key] = self.sem_pool[len(sems)]
            return sems[key]
        for e, c in cnt.items():
            for ep in range((c + self.SEM_LIMIT - 1) // self.SEM_LIMIT):
                getsem((e, ep))
        for k in self.dma_count:
            getsem(("dma", k))
        progs = {}
        waited = {}
        for i, o in enumerate(ops):
            e = o["eng"]
            wl = {}
            for d in sorted(o["deps"]):
                po = ops[d]
                if po["dma"] is not None:
                    key = ("dma", po["dma"][0]); val = po["dval"]
                    order = (0, val)
                else:
                    if po["eng"] == "pe" and e == "pe":
                        continue
                    pe_, ep, v = po["sv"]
                    key = (pe_, ep); val = v
                    order = (ep, v)
                stream = key if key[0] == "dma" else key[0]
                w = waited.setdefault(e, {})
                if stream in w and w[stream] >= order:
                    continue
                w[stream] = order
                wl[stream] = (key, val)
            progs.setdefault(e, []).append((list(wl.values()), o))
        self.n_sems = len(sems)

        self.trace = {e: [([ (k, v) for k, v in w], o.get("sv"), o.get("dval"), o["dma"]) for w, o in items] for e, items in progs.items()}

        def run(eng, items):
            for waits, o in items:
                for key, val in waits:
                    eng.wait_ge(sems[key], val)
                inst = o["fn"](eng, sems[("dma", o["dma"][0])] if o["dma"] is not None else None)
                if o["dma"] is None and o["sig"]:
                    e_, ep, v = o["sv"]
                    inst.then_inc(sems[(e_, ep)], 1)

        if "pe" in progs:
            @block.tensor
            def _(t):
                run(t, progs["pe"])
        if "act" in progs:
            @block.scalar
            def _(a):
                run(a, progs["act"])
        if "dve" in progs:
            @block.vector
            def _(v):
                run(v, progs["dve"])
        if "pool" in progs:
            @block.gpsimd
            def _(g):
                run(g, progs["pool"])
        if "sp" in progs:
            @block.sync
            def _(s):
                run(s, progs["sp"])


def build_program(stop=None, dbg_off=0, dbg_n=2048):
    nc = bass.Bass("TRN2", target_bir_lowering=False)

    def din(name, shape):
        return nc.dram_tensor(name, list(shape), F32, kind="ExternalInput").ap()
    xs = din("xs", [S, D])
    xh = din("xh", [128, D])
    mem = din("mem", [256, D])
    w_in = din("w_in", [D, IN_W])
    w_mkv = din("w_mkv", [D, D])
    w_gate = din("w_gate", [D, 3 * D])
    w_br = din("w_br", [3, 1024, D])
    w_o = din("w_o", [D, D])
    w_up = din("w_up", [D, 2 * D_FF])
    w_down = din("w_down", [D_FF, D])
    pp_d = din("pp", [128, PP_N])
    tzd_d = din("tzd", [128, 8, 1068])
    cnd_d = din("cnd", [128, 8, 2, 128])
    hbd_d = din("hbd", [128, 8, 16, 2])
    tzw_d = din("tzw", [128, 2, 3, 512])
    tze_d = din("tze", [128, 2, 2, 512])
    hbw_d = din("hbw", [128, 2, 2, 3, 4])
    y = nc.dram_tensor("y", [T, D], F32, kind="ExternalOutput").ap()
    dbg = nc.dram_tensor("dbg", [128, dbg_n], F32, kind="ExternalOutput").ap() if stop else None

    stack = ExitStack()
    XW = 43164
    big = stack.enter_context(nc.sbuf_tensor("big", [128, XW], F32))
    wr = stack.enter_context(nc.sbuf_tensor("wr", [128, NSLOT * KC * WCOLS], BF16))
    cst = stack.enter_context(nc.sbuf_tensor("cst", [128, 1536], F32))
    ps = stack.enter_context(nc.psum_tensor("ps", [128, 8, 512], F32))
    sem_pool = [stack.enter_context(nc.semaphore("s%d" % i)) for i in range(40)]
    block = stack.enter_context(nc.Block())
    sc = Sched()
    sc.sem_pool = sem_pool

    def f32v(off, n):
        return big[:, off:off + n]

    def bfv(off, nbf):
        assert nbf % 2 == 0
        return big[:, off:off + nbf // 2].bitcast(BF16)

    pp = cst[:, 0:PP_N]
    ident_f = cst[:, 800:928]
    ones_avg = cst[:, 928:1056]
    ones_f = cst[:, 1056:1184]
    ident_b = cst[:, 1184:1248].bitcast(BF16)
    ones_b = cst[:, 1248:1312].bitcast(BF16)
    misc = cst[:, 1312:1536]
    neglam = misc[:, 0:1]
    gcol = misc[:, 1:2]
    epsc = misc[:, 2:3]
    esink = misc[:, 3:11]
    lsum = misc[:, 11:13]
    lexp = misc[:, 13:15]
    zcol = misc[:, 15:16]
    stat_all = misc[:, 16:16 + 18 * 2].rearrange("p (t k) -> p t k", k=2)
    stat_sc = misc[:, 52:52 + 18 * 2].rearrange("p (t k) -> p t k", k=2)
    bnst = misc[:, 96:96 + 48].rearrange("p (s k) -> p s k", s=2)
    ag0 = misc[:, 144:160]
    ab0 = misc[:, 160:176]
    ag1 = misc[:, 176:192]
    ab1 = misc[:, 192:208]
    lntmp = misc[:, 208:224]

    def ppc(off, n=1):
        return pp[:, off:off + n]

    def finish_dbg():
        def fnd(e, sem):
            i = e.dma_start(out=dbg, in_=big[:, dbg_off:dbg_off + dbg_n])
            i.then_inc(sem, 16)
            return i
        sc.op("sp", fnd, reads=[], writes=["dbgout"], dma=("dbgout", 1))
        sc.op("sp", lambda e, _s: e.nop(), reads=["dbgout"], writes=["done"])
        sc.emit(nc, stack, block)
        stack.close()
        nc._sched = sc
        return nc

    rr = {"ps": 0, "slot": 0, "psmod": 8}
    slot_uses = [0] * NSLOT

    def wslot(i):
        return wr[:, i * KC * WCOLS:(i + 1) * KC * WCOLS].rearrange("p (c n) -> p c n", c=KC)

    def load_w(pieces):
        si = rr["slot"] % NSLOT
        rr["slot"] += 1
        sl = wslot(si)

        def fn(eng, sem, pieces=pieces, sl=sl):
            inst = None
            for src, off in pieces:
                rows, ncols = src.shape
                nk = rows // 128
                inst = eng.dma_start(out=sl[:, 0:nk, off:off + ncols],
                                     in_=src.rearrange("(c p) n -> p c n", p=128))
                inst.then_inc(sem, 16)
            return inst
        sc.op("pool", fn, writes=["w%d" % si], dma=("w%d" % si, len(pieces)), nobar=True)
        return si

    def dma_in(dst, src, key, semkey, eng="sp"):
        def fn(e, sem, dst=dst, src=src):
            inst = e.dma_start(out=dst, in_=src)
            inst.then_inc(sem, 16)
            return inst
        sc.op(eng, fn, writes=[key], dma=(semkey, 1))

    def mm_group(out, pairs, reads, pskey):
        def fn(eng, _s, out=out, pairs=pairs):
            inst = None
            n = len(pairs)
            for i, (l, r) in enumerate(pairs):
                inst = eng.matmul(out, l, r, start=(i == 0), stop=(i == n - 1))
            return inst
        sc.op("pe", fn, reads=reads, writes=[pskey])

    def mm_one(out, l, r, start, stop, reads, pskey):
        def fn(eng, _s):
            return eng.matmul(out, l, r, start=start, stop=stop)
        sc.op("pe", fn, reads=reads, writes=[pskey])

    def tr_group(outs_ins, ident, reads, pskey):
        def fn(eng, _s):
            inst = None
            for o, i_ in outs_ins:
                inst = eng.transpose(o, i_, ident)
            return inst
        sc.op("pe", fn, reads=reads, writes=[pskey])

    def act(out, in_, func, reads, writes, bias=None, scale=None):
        def fn(eng, _s):
            kw = {}
            if bias is not None:
                kw["bias"] = bias
            if scale is not None:
                kw["scale"] = scale
            return eng.activation(out, in_, func, **kw)
        sc.op("act", fn, reads=reads, writes=writes)

    def dve(fn, reads, writes, eng="dve"):
        sc.op(eng, lambda e, _s: fn(e), reads=reads, writes=writes)

    def copy(out, in_, reads, writes, eng="dve"):
        if eng == "act":
            act(out, in_, AF.Copy, reads, writes)
        else:
            dve(lambda e: e.tensor_copy(out, in_), reads, writes, eng=eng)

    def psbank():
        b = rr["ps"] % rr["psmod"]
        rr["ps"] += 1
        return b

    cm_d = din("cm", [128, 128])
    dma_in(pp, pp_d, "pp", "c0")
    dma_in(ident_f, cm_d, "ident_f", "c1")
    dve(lambda e: e.memset(ones_avg, 1.0 / 128.0), [], ["ones_avg"])
    dve(lambda e: e.memset(ones_f, 1.0), [], ["ones_f"])
    dve(lambda e: e.memset(ones_b, 1.0), [], ["ones_b"])
    dve(lambda e: e.memset(epsc, LN_EPS), [], ["epsc"])
    dve(lambda e: e.tensor_copy(ident_b, ident_f), ["ident_f"], ["ident_b"])
    dve(lambda e: e.tensor_scalar(gcol, ppc(PP_SG), 1.0 - LAMBDA_INIT, None, ALU.mult), ["pp"], ["gcol"])
    t64 = f32v(40900, 64)
    for i in range(2):
        dve(lambda e, i=i: e.tensor_tensor(t64, ppc(PP_LQ + 128 * i, 64), ppc(PP_LQ + 128 * i + 64, 64), ALU.mult),
            ["pp"], ["t64"])
        dve(lambda e, i=i: e.reduce_sum(lsum[:, i:i + 1], t64, AX.X), ["t64"], ["lsum%d" % i])
    act(lexp, lsum, AF.Exp, ["lsum0", "lsum1"], ["lexp"])
    dve(lambda e: e.tensor_tensor(neglam, lexp[:, 1:2], lexp[:, 0:1], ALU.subtract), ["lexp"], ["neglam0"])
    dve(lambda e: e.tensor_scalar(neglam, neglam, -LAMBDA_INIT, None, ALU.add), ["neglam0"], ["neglam"])
    act(esink, ppc(PP_SINK, 8), AF.Exp, ["pp"], ["esink"])
    for dst, off, nm in ((ag0, 0, "ag0"), (ab0, 16, "ab0"), (ag1, 32, "ag1"), (ab1, 48, "ab1")):
        dve(lambda e, dst=dst, off=off: e.tensor_scalar(dst, ppc(PP_LN + off, 16), ALPHA, None, ALU.mult),
            ["pp"], [nm])
    CONSTK = ["pp", "ident_f", "ident_b", "ones_avg", "ones_f", "ones_b", "epsc", "gcol", "neglam", "esink",
              "ag0", "ab0", "ag1", "ab1"]

    hT_own = bfv(0, 16 * 1028).rearrange("p (c n) -> p c n", c=16)
    hT_oth = bfv(8224, 16 * 1024).rearrange("p (c n) -> p c n", c=16)
    def HKT(t):
        return ["hT%d_%d" % (t, dc) for dc in range(16)]
    HK_OWN = [k for t in range(8) for k in HKT(t)] + ["hTh"]
    HK_OTH = [k for t in range(8, 16) for k in HKT(t)]

    def hT_tile(t):
        return hT_own[:, :, t * 128:(t + 1) * 128] if t < 8 else hT_oth[:, :, (t - 8) * 128:(t - 7) * 128]

    def hT_chunk512(tc):
        return hT_own[:, :, tc * 512:(tc + 1) * 512] if tc < 2 else hT_oth[:, :, (tc - 2) * 512:(tc - 1) * 512]

    def ln_stats(xt, t, xkey):
        sl = t % 2
        for s4 in range(4):
            dve(lambda e, s4=s4: e.bn_stats(bnst[:, sl, s4 * 6:(s4 + 1) * 6], xt[:, s4 * 512:(s4 + 1) * 512]),
                [xkey], ["bn%d_%d" % (sl, s4)])
        dve(lambda e: e.bn_aggr(stat_all[:, t, :], bnst[:, sl, :]), ["bn%d_%d" % (sl, s) for s in range(4)],
            ["st%d" % t])
        act(stat_sc[:, t, 0:1], stat_all[:, t, 1:2], AF.Ln, ["st%d" % t, "epsc"], ["sca%d" % t], bias=epsc)
        act(stat_sc[:, t, 0:1], stat_sc[:, t, 0:1], AF.Exp, ["sca%d" % t], ["scb%d" % t], scale=-0.5)
        dve(lambda e: e.scalar_tensor_tensor(stat_sc[:, t, 1:2], stat_all[:, t, 0:1], -1.0, stat_sc[:, t, 0:1],
                                             ALU.mult, ALU.mult), ["st%d" % t, "scb%d" % t], ["sc%d" % t])

    evc = {"n": 0}

    def evac(dst, src, reads, writes, eng=None):
        if eng is None:
            eng = "dve" if evc["n"] % 2 == 0 else "act"
            evc["n"] += 1
        copy(dst, src, reads, writes, eng=eng)

    xtA = [f32v(16416, 2048), f32v(18464, 2048)]
    xhA = [bfv(20512, 2048), bfv(21536, 2048)]
    dv_g = bfv(28728, 16 * 512).rearrange("p (c n) -> p c n", c=16)

    def dv_load(g):
        if rr["slot"] % NSLOT % 2 == 1:
            rr["slot"] += 1
        s0 = load_w([(w_in[:, 2048 + g * 512:2048 + g * 512 + 256], 0)])
        s1 = load_w([(w_in[:, 2048 + g * 512 + 256:2048 + g * 512 + 512], 0)])
        assert s1 == s0 + 1
        return (s0, s1)

    def dv_piece(kt, sv):
        s0, s1 = sv
        b = psbank()
        ht = hT_tile(kt)
        w2 = wr[:, s0 * KC * WCOLS:(s0 + 2) * KC * WCOLS].rearrange("p (s c n) -> p s c n", s=2, c=KC)
        mm_group(ps[:, b, :].rearrange("p (s n) -> p s n", s=2), [(ht[:, kc, :], w2[:, :, kc, :]) for kc in range(KC)],
                 HKT(kt) + ["w%d" % s0, "w%d" % s1], "ps%d" % b)
        evac(dv_g[:, kt, :], ps[:, b, :], ["ps%d" % b], ["dv%d" % kt])
    dvw0 = dv_load(0)
    def a_stage1(t):
        sl = t % 2
        dma_in(xtA[sl], xs[t * 128:(t + 1) * 128, :], "xt%d" % sl, "xt%d" % sl)
        ln_stats(xtA[sl], t, "xt%d" % sl)
        act(xhA[sl], xtA[sl], AF.Identity, ["xt%d" % sl, "sc%d" % t, "scb%d" % t], ["xh%d" % sl],
            bias=stat_sc[:, t, 1:2], scale=stat_sc[:, t, 0:1])

    def a_stage2(t):
        sl = t % 2
        for g4 in range(4):
            b = psbank()
            pb = ps[:, b, :].bitcast(BF16)
            tr_group([(pb[:, j * 128:(j + 1) * 128], xhA[sl][:, (g4 * 4 + j) * 128:(g4 * 4 + j + 1) * 128])
                      for j in range(4)], ident_b, ["xh%d" % sl, "ident_b"], "ps%d" % b)
            for j in range(4):
                dc = g4 * 4 + j
                dst = hT_tile(t)[:, dc, :]
                src = pb[:, j * 128:(j + 1) * 128]
                if g4 % 2 == 0:
                    act(dst, src, AF.Identity, ["ps%d" % b, "pp"], ["hT%d_%d" % (t, dc)],
                        bias=ppc(PP_LN + 16 + dc), scale=ppc(PP_LN + dc))
                else:
                    dve(lambda e, dst=dst, src=src, dc=dc: e.tensor_scalar(
                        dst, src, ppc(PP_LN + dc), ppc(PP_LN + 16 + dc), ALU.mult, ALU.add),
                        ["ps%d" % b, "pp"], ["hT%d_%d" % (t, dc)])
    a_stage1(0)
    for t in range(16):
        if t + 1 < 16:
            a_stage1(t + 1)
        a_stage2(t)
        dv_piece(t, dvw0)
    dve(lambda e: e.tensor_copy(hT_own[:, :, 1024:1025], hT_oth[:, :, 1023:1024]), HKT(15), ["hTh"])
    dve(lambda e: e.tensor_copy(hT_own[:, :, 1025:1026], hT_oth[:, :, 0:1]), HKT(8) + ["hTh"], ["hTh"])
    sc.barrier()
    if stop == 'A':
        return finish_dbg()

    aT = bfv(16416, 8 * NQ).rearrange("p (c n) -> p c n", c=8)
    bT = bfv(16416 + 4104, 8 * NQ).rearrange("p (c n) -> p c n", c=8)
    cT = bfv(16416 + 8208, 8 * NQ).rearrange("p (c n) -> p c n", c=8)

    def proj_fm(si, coff, rhs_of_chunk, nchunks, width, dst_of_chunk, reads, wkey, nk=KC, eng=None):
        sl = wslot(si)
        for c in range(nchunks):
            b = psbank()
            rhs = rhs_of_chunk(c)
            mm_group(ps[:, b, 0:width], [(sl[:, kc, coff:coff + 128], rhs[:, kc, :]) for kc in range(nk)],
                     reads + ["w%d" % si], "ps%d" % b)
            evac(dst_of_chunk(c), ps[:, b, 0:width], ["ps%d" % b], [wkey + "_%d" % c], eng=eng)

    dkT = [bfv(32824, 2048), bfv(33848, 2048)]
    dqT = [bfv(34872, 1028)[:, 0:NQ], bfv(35386, 1028)[:, 0:NQ]]
    biasb = [(f32v(35900, 1068), f32v(36968, 256).rearrange("p (a n) -> p a n", a=2),
              f32v(43048, 32).rearrange("p (a n) -> p a n", a=16)),
             (f32v(41724, 1068), f32v(42792, 256).rearrange("p (a n) -> p a n", a=2),
              f32v(43080, 32).rearrange("p (a n) -> p a n", a=16))]
    Pb = [bfv(37244, 684).rearrange("p (a n) -> p a n", a=2), bfv(37586, 684).rearrange("p (a n) -> p a n", a=2)]
    tmpb = [f32v(37928, 684).rearrange("p (a n) -> p a n", a=2), f32v(38612, 684).rearrange("p (a n) -> p a n", a=2)]
    R1 = f32v(39296, 342); R2 = f32v(39638, 342); A_ = f32v(39980, 342); SQd = f32v(40322, 342); RSd = f32v(40664, 342)
    SC_D = 64 ** -0.5
    RR = f32v(41040, 684).rearrange("p (a n) -> p a n", a=2)

    def diff_load(h):
        si = load_w([(w_in[:, h * 128:(h + 1) * 128], 0), (w_in[:, 1024 + h * 128:1024 + (h + 1) * 128], 128)])
        tz, cn, hb = biasb[h % 2]

        def fn(e, sem):
            e.dma_start(out=tz, in_=tzd_d[:, h, :]).then_inc(sem, 16)
            e.dma_start(out=cn, in_=cnd_d[:, h, :, :]).then_inc(sem, 16)
            i3 = e.dma_start(out=hb, in_=hbd_d[:, h, :, :])
            i3.then_inc(sem, 16)
            return i3
        sc.op("sp", fn, writes=["bias%d" % (h % 2)], dma=("bias%d" % (h % 2), 3))
        return si

    def diff_segments(h, ktg, c0, w):
        tz, cn, hb = biasb[h % 2]
        segs = []
        own_hi = min(c0 + w, 1024)

        def add(lo, hi, kind, data):
            lo = max(lo, c0); hi = min(hi, own_hi)
            if hi > lo:
                segs.append((lo - c0, hi - c0, kind, data(lo, hi)))
        if ktg < 8:
            kt = ktg
            lb, hb_ = 128 * (kt - 1), 128 * (kt + 2)
            if lb < own_hi and hb_ > c0:
                m0 = c0 - 128 * kt + 128 + 342
                segs.append((0, own_hi - c0, "tile", tz[:, m0:m0 + own_hi - c0]))
            else:
                add(0, lb, "const", lambda lo, hi: ppc(PP_CHI + h))
                add(hb_, 1024, "const", lambda lo, hi: ppc(PP_CLO + h))
        else:
            kp = ktg - 8
            if kp == 0:
                add(0, 896, "const", lambda lo, hi: ppc(PP_COT + h))
                add(896, 1024, "tile", lambda lo, hi: cn[:, 0, lo - 896:hi - 896])
            elif kp == 7:
                add(0, 128, "tile", lambda lo, hi: cn[:, 1, lo:hi])
                add(128, 1024, "const", lambda lo, hi: ppc(PP_COT + h))
            else:
                add(0, 1024, "const", lambda lo, hi: ppc(PP_COT + h))
        if c0 + w > 1024:
            for jj in range(2):
                segs.append((1024 + jj - c0, 1025 + jj - c0, "const0", hb[:, ktg, jj:jj + 1]))
        return segs

    pairc = {"n": 0, "p": 0, "t": 0}

    def diff_attention(h, g, hooks={}):
        hl = h % 4
        bf = h % 2
        bkey = "bias%d" % bf
        tasks = [(c, ktg) for c in range(3) for ktg in range(16)]
        base = pairc["n"]
        pairc["n"] += len(tasks) + 1
        QB = [(0, 342), (342, 342), (684, 340)]

        def p0_of(i):
            return 2 * ((base + i) % 2)

        def emit_qk(i):
            c, ktg = tasks[i]
            c0, w = QB[c]
            p0 = p0_of(i)

            def qk(eng, _s, p0=p0, ktg=ktg, c0=c0, w=w):
                eng.matmul(ps[:, p0, 0:w], dkT[bf][0:64, ktg * 128:(ktg + 1) * 128], dqT[bf][0:64, c0:c0 + w],
                           start=True, stop=True)
                return eng.matmul(ps[:, p0 + 1, 0:w], dkT[bf][64:128, ktg * 128:(ktg + 1) * 128],
                                  dqT[bf][64:128, c0:c0 + w], start=True, stop=True)
            sc.op("pe", qk, reads=["dk%d_%d" % (bf, ktg // 4), "dq%d_%d" % (bf, c)],
                  writes=["ps%d" % p0, "ps%d" % (p0 + 1)])

        def emit_exp(i):
            c, ktg = tasks[i]
            c0, w = QB[c]
            p0 = p0_of(i)
            pi = (base + i) % 2
            P = Pb[pi]
            pkeys = []
            for si_, (lo, hi, kind, data) in enumerate(diff_segments(h, ktg, c0, w)):
                pk = "P%d_%d" % (pi, si_)
                if kind in ("const", "const0"):
                    act(P[:, :, lo:hi], ps[:, p0:p0 + 2, lo:hi], AF.Exp, ["ps%d" % p0, "ps%d" % (p0 + 1), "pp", bkey],
                        [pk], bias=data, scale=SC_D)
                    pkeys.append(pk)
                else:
                    pkeys.append(pk)
                    ti = pairc["t"] % 2; pairc["t"] += 1
                    tm = tmpb[ti]
                    for m in range(2):
                        dve(lambda e, m=m, tm=tm, lo=lo, hi=hi, data=data, p0=p0: e.scalar_tensor_tensor(
                            tm[:, m, lo:hi], ps[:, p0 + m, lo:hi], SC_D, data, ALU.mult, ALU.add),
                            ["ps%d" % (p0 + m), bkey], ["tmp%d_%d" % (ti, m)])
                    act(P[:, :, lo:hi], tm[:, :, lo:hi], AF.Exp, ["tmp%d_0" % ti, "tmp%d_1" % ti], [pk])
            return P, pkeys

        def emit_pv(i, P, pkeys):
            c, ktg = tasks[i]
            w = QB[c][1]

            def pv(eng, _s, P=P, ktg=ktg, w=w):
                st, sp_ = (ktg == 0), (ktg == 15)
                vv = dv_g[:, ktg, hl * 128:(hl + 1) * 128]
                eng.matmul(ps[:, 4, 0:w], vv, P[:, 0, 0:w], start=st, stop=sp_)
                eng.matmul(ps[:, 5, 0:w], vv, P[:, 1, 0:w], start=st, stop=sp_)
                eng.matmul(ps[:, 6, 0:w], ones_b, P[:, 0, 0:w], start=st, stop=sp_)
                return eng.matmul(ps[:, 7, 0:w], ones_b, P[:, 1, 0:w], start=st, stop=sp_)
            sc.op("pe", pv, reads=pkeys + ["dv%d" % ktg, "ones_b"], writes=["ps4", "ps5", "ps6", "ps7"])

        def fin_head(c):
            w = QB[c][1]
            dve(lambda e: e.tensor_copy(R1[:, 0:w], ps[:, 4, 0:w]), ["ps4"], ["R1"])
            dve(lambda e: e.tensor_copy(R2[:, 0:w], ps[:, 5, 0:w]), ["ps5"], ["R2"])
            act(RR[:, :, 0:w], ps[:, 6:8, 0:w], AF.Ln, ["ps6", "ps7"], ["RR"])

        def fin_mid(c):
            c0, w = QB[c]
            act(RR[:, :, 0:w], RR[:, :, 0:w], AF.Exp, ["RR"], ["RR"], scale=-1.0)
            dve(lambda e: e.tensor_tensor(R1[:, 0:w], R1[:, 0:w], RR[:, 0, 0:w], ALU.mult), ["R1", "RR"], ["R1"])
            dve(lambda e: e.tensor_tensor(R2[:, 0:w], R2[:, 0:w], RR[:, 1, 0:w], ALU.mult), ["R2", "RR"], ["R2"])
            dve(lambda e: e.scalar_tensor_tensor(A_[:, 0:w], R2[:, 0:w], neglam, R1[:, 0:w], ALU.mult, ALU.add),
                ["R1", "R2", "neglam"], ["A_"])
            dve(lambda e: e.tensor_tensor(SQd[:, 0:w], A_[:, 0:w], A_[:, 0:w], ALU.mult), ["A_"], ["SQd"])

        def fin_pe(c, p0):
            c0, w = QB[c]
            mm_one(ps[:, p0, 0:w], ones_avg, SQd[:, 0:w], True, True, ["SQd", "ones_avg"], "ps%d" % p0)
            act(RSd[:, 0:w], ps[:, p0, 0:w], AF.Ln, ["ps%d" % p0, "epsc"], ["RSd"], bias=epsc)
            act(RSd[:, 0:w], RSd[:, 0:w], AF.Exp, ["RSd"], ["RSd"], scale=-0.5)
            dve(lambda e: e.scalar_tensor_tensor(aT[:, h, c0:c0 + w], A_[:, 0:w], gcol, RSd[:, 0:w],
                                                 ALU.mult, ALU.mult), ["A_", "RSd", "gcol"], ["aT"])

        emit_qk(0)
        pend_mid = None
        pend_pe = None
        for i in range(len(tasks)):
            if i + 1 < len(tasks):
                emit_qk(i + 1)
            P, pkeys = emit_exp(i)
            emit_pv(i, P, pkeys)
            if pend_mid is not None:
                fin_mid(pend_mid)
                pend_pe = (pend_mid, i + 4)
                pend_mid = None
            if pend_pe is not None and i >= pend_pe[1]:
                fin_pe(pend_pe[0], p0_of(i))
                pend_pe = None
            if tasks[i][1] == 15:
                fin_head(tasks[i][0])
                pend_mid = tasks[i][0]
            if i in hooks:
                hooks[i](p0_of(i))
        fin_mid(pend_mid)
        fin_pe(pend_mid, p0_of(len(tasks) - 1))
        p0 = p0_of(len(tasks))
        tz_, cn_, hb_ = biasb[bf]

        def qkh(eng, _s):
            inst = None
            for ktg in range(16):
                for m in range(2):
                    inst = eng.matmul(ps[:, p0, ktg * 4 + m * 2:ktg * 4 + m * 2 + 2],
                                      dkT[bf][64 * m:64 * m + 64, ktg * 128:(ktg + 1) * 128],
                                      dqT[bf][64 * m:64 * m + 64, 1024:1026], start=True, stop=True)
            return inst
        sc.op("pe", qkh, reads=["dk%d_%d" % (bf, a) for a in range(4)] + ["dq%d_2" % bf], writes=["ps%d" % p0])
        tmh = tmpb[0][:, 0, 0:64]
        Phh = Pb[0][:, 0, 0:64]
        for m in range(2):
            dve(lambda e, m=m: e.scalar_tensor_tensor(
                tmh.rearrange("p (k m j) -> p k m j", k=16, m=2)[:, :, m, :],
                ps[:, p0, 0:64].rearrange("p (k m j) -> p k m j", k=16, m=2)[:, :, m, :], SC_D, hb_,
                ALU.mult, ALU.add), ["ps%d" % p0, bkey], ["tmp0_%d" % m])
        act(Phh, tmh, AF.Exp, ["tmp0_0", "tmp0_1"], ["P0_0"])
        Ph4 = Phh.rearrange("p (k m j) -> p k m j", k=16, m=2)

        def pvh(eng, _s):
            inst = None
            for gi_ in range(4):
                m = gi_ % 2
                for ktg in range(16):
                    lhs = dv_g[:, ktg, hl * 128:(hl + 1) * 128] if gi_ < 2 else ones_b
                    inst = eng.matmul(ps[:, p0 + 1, gi_ * 2:gi_ * 2 + 2], lhs, Ph4[:, ktg, m, :],
                                      start=(ktg == 0), stop=(ktg == 15))
            return inst
        sc.op("pe", pvh, reads=["P0_0", "ones_b"] + ["dv%d" % k for k in range(16)], writes=["ps%d" % (p0 + 1)])
        pa = ps[:, p0 + 1, :]
        pk_ = "ps%d" % (p0 + 1)
        dve(lambda e: e.reciprocal(R1[:, 0:2], pa[:, 4:6]), [pk_], ["R1"])
        dve(lambda e: e.reciprocal(R2[:, 0:2], pa[:, 6:8]), [pk_], ["R2"])
        dve(lambda e: e.tensor_tensor(R1[:, 0:2], pa[:, 0:2], R1[:, 0:2], ALU.mult), [pk_, "R1"], ["R1"])
        dve(lambda e: e.tensor_tensor(R2[:, 0:2], pa[:, 2:4], R2[:, 0:2], ALU.mult), [pk_, "R2"], ["R2"])
        dve(lambda e: e.scalar_tensor_tensor(A_[:, 0:2], R2[:, 0:2], neglam, R1[:, 0:2], ALU.mult, ALU.add),
            ["R1", "R2", "neglam"], ["A_"])
        dve(lambda e: e.tensor_tensor(SQd[:, 0:2], A_[:, 0:2], A_[:, 0:2], ALU.mult), ["A_"], ["SQd"])
        mm_one(ps[:, p0, 0:2], ones_avg, SQd[:, 0:2], True, True, ["SQd", "ones_avg"], "ps%d" % p0)
        act(RSd[:, 0:2], ps[:, p0, 0:2], AF.Ln, ["ps%d" % p0, "epsc"], ["RSd"], bias=epsc)
        act(RSd[:, 0:2], RSd[:, 0:2], AF.Exp, ["RSd"], ["RSd"], scale=-0.5)
        dve(lambda e: e.scalar_tensor_tensor(aT[:, h, 1024:1026], A_[:, 0:2], gcol, RSd[:, 0:2],
                                             ALU.mult, ALU.mult), ["A_", "RSd", "gcol"], ["aT"])

    def diff_project_pieces(h, si):
        bf = h % 2
        sl = wslot(si)
        pieces = []
        for c in range(4):
            def pc(b=None, c=c):
                if b is None:
                    b = psbank()
                rhs = hT_chunk512(c)
                mm_group(ps[:, b, 0:512], [(sl[:, kc, 128:256], rhs[:, kc, :]) for kc in range(KC)],
                         HK_OWN + HK_OTH + ["w%d" % si], "ps%d" % b)
                evac(dkT[bf][:, c * 512:(c + 1) * 512], ps[:, b, 0:512], ["ps%d" % b], ["dk%d_%d" % (bf, c)])
            pieces.append(pc)
        for c in range(3):
            def pq(b=None, c=c):
                if b is None:
                    b = psbank()
                c0 = QCH[c][0]
                mm_group(ps[:, b, 0:342], [(sl[:, kc, 0:128], hT_own[:, kc, c0:c0 + 342]) for kc in range(KC)],
                         HK_OWN + ["w%d" % si], "ps%d" % b)
                evac(dqT[bf][:, c0:c0 + 342], ps[:, b, 0:342], ["ps%d" % b], ["dq%d_%d" % (bf, c)])
            pieces.append(pq)
        return pieces

    rr["psmod"] = 4
    nxt = diff_load(0)
    pend_pieces = diff_project_pieces(0, nxt)
    for g in range(2):
        if g == 1:
            dvw1 = dv_load(1)
            for kt in range(16):
                dv_piece(kt, dvw1)
        for hl in range(4):
            h = 4 * g + hl
            for pcs_ in pend_pieces:
                pcs_()
            pend_pieces = []
            if stop == 'B1':
                return finish_dbg()
            hooks = {}
            if h < 7:
                nxt = diff_load(h + 1)
                pend_pieces = diff_project_pieces(h + 1, nxt)
            try:
                diff_attention(h, g, hooks)
            except _Cut:
                return finish_dbg()
            if stop == 'B2':
                return finish_dbg()
    sc.barrier()
    if stop == 'B':
        return finish_dbg()

    wqT_g = bfv(28728, 4 * NQ).rearrange("p (c n) -> p c n", c=4)
    wkT_all = bfv(39648, 4096).rearrange("p (g n) -> p g n", g=2)
    wv_all = bfv(37600, 4096).rearrange("p (c n) -> p c n", c=16)
    tzw_g = f32v(32828, 1536).rearrange("p (a n) -> p a n", a=3)
    tze_g = f32v(34364, 1024).rearrange("p (a n) -> p a n", a=2)
    hbw_g = f32v(35388, 24).rearrange("p (a b c) -> p a b c", a=2, b=3)
    Pw = [bfv(35412, 512), bfv(35668, 512)]
    tmpw = [f32v(35924, 512), f32v(36436, 512)]
    DEN = f32v(36948, 512)
    tmph = f32v(37460, 4)
    DENh = f32v(37464, 4)
    Ph = bfv(37500, 8)[:, 0:4]
    SC_W = 128 ** -0.5
    skA = load_w([(w_in[:, 4096:4352], 0)])
    skB = load_w([(w_in[:, 4352:4608], 0)])
    for gg in range(2):
        proj_fm(skA, gg * 128, hT_chunk512, 4, 512, lambda c, gg=gg: wkT_all[:, gg, c * 512:(c + 1) * 512],
                HK_OWN + HK_OTH, "wk%d" % gg)
    slB = wslot(skB)
    for kt in range(16):
        b = psbank()
        ht = hT_tile(kt)
        mm_group(ps[:, b, 0:256], [(ht[:, kc, :], slB[:, kc, 0:256]) for kc in range(KC)],
                 HKT(kt) + ["w%d" % skB], "ps%d" % b)
        evac(wv_all[:, kt, :], ps[:, b, 0:256], ["ps%d" % b], ["wv%d" % kt])
    for g in range(2):
        wkT_g = wkT_all[:, g, :]
        wv_g = wv_all[:, :, g * 128:(g + 1) * 128]

        def fnb(e, sem, g=g):
            e.dma_start(out=tzw_g, in_=tzw_d[:, g, :, :]).then_inc(sem, 16)
            e.dma_start(out=tze_g, in_=tze_d[:, g, :, :]).then_inc(sem, 16)
            i3 = e.dma_start(out=hbw_g, in_=hbw_d[:, g, :, :, :])
            i3.then_inc(sem, 16)
            return i3
        sc.op("sp", fnb, writes=["wbias"], dma=("wbias", 3))
        sq0 = load_w([(w_in[:, 3072 + g * 512:3072 + g * 512 + 256], 0)])
        sq1 = load_w([(w_in[:, 3072 + g * 512 + 256:3072 + g * 512 + 512], 0)])
        for gh in range(4):
            sq = sq0 if gh < 2 else sq1
            proj_fm(sq, (gh % 2) * 128, lambda c: hT_own[:, :, QCH[c][0]:QCH[c][0] + 342], 3, 342,
                    lambda c, gh=gh: wqT_g[:, gh, QCH[c][0]:QCH[c][0] + 342], HK_OWN, "wq%d" % gh)
        WQK = ["wq%d_%d" % (a, c) for a in range(4) for c in range(3)]
        wtasks = []
        for qt in range(8):
            for ti, d in enumerate((-1, 0, 1)):
                if qt == 0 and d == -1:
                    wtasks.append((qt, ti, 15, tze_g[:, 0, :]))
                elif qt == 7 and d == 1:
                    wtasks.append((qt, ti, 8, tze_g[:, 1, :]))
                else:
                    wtasks.append((qt, ti, qt + d, tzw_g[:, d + 1, :]))
        wbanks = [psbank() for _ in wtasks]

        def w_qk(i):
            qt, ti, ktg, btile = wtasks[i]
            b = wbanks[i]
            mm_one(ps[:, b, :].rearrange("p (a n) -> p a n", a=4), wkT_g[:, ktg * 128:(ktg + 1) * 128],
                   wqT_g[:, :, qt * 128:(qt + 1) * 128], True, True, ["wk%d_%d" % (g, ktg // 4)] + WQK, "ps%d" % b)
        w_qk(0)
        w_qk(1)
        for i, (qt, ti, ktg, btile) in enumerate(wtasks):
            if i + 2 < len(wtasks):
                w_qk(i + 2)
            b = wbanks[i]
            pi = i % 2
            ob = 4 + (qt % 2)
            sb = 6 + (qt % 2)
            dve(lambda e, b=b, pi=pi, btile=btile: e.scalar_tensor_tensor(
                tmpw[pi], ps[:, b, :], SC_W, btile, ALU.mult, ALU.add), ["ps%d" % b, "wbias"], ["tmpw%d" % pi])
            act(Pw[pi], tmpw[pi], AF.Exp, ["tmpw%d" % pi], ["Pw%d" % pi])

            def pvw(eng, _s, ti=ti, ktg=ktg, pi=pi, ob=ob, sb=sb, wv_g=wv_g):
                eng.matmul(ps[:, ob, :], wv_g[:, ktg, :], Pw[pi], start=(ti == 0), stop=(ti == 2))
                return eng.matmul(ps[:, sb, :], ones_b, Pw[pi], start=(ti == 0), stop=(ti == 2))
            sc.op("pe", pvw, reads=["Pw%d" % pi, "wv%d" % ktg, "ones_b"], writes=["ps%d" % ob, "ps%d" % sb])
            if ti == 2:
                for gh in range(4):
                    act(DEN[:, gh * 128:(gh + 1) * 128], ps[:, sb, gh * 128:(gh + 1) * 128], AF.Ln,
                        ["ps%d" % sb, "esink"], ["DEN%d" % gh], bias=esink[:, 4 * g + gh:4 * g + gh + 1])
                act(DEN, DEN, AF.Exp, ["DEN%d" % a for a in range(4)], ["DENr"], scale=-1.0)
                dve(lambda e, ob=ob, qt=qt, g=g: e.tensor_tensor(
                    bT[:, 4 * g:4 * g + 4, qt * 128:(qt + 1) * 128], ps[:, ob, :].rearrange("p (a n) -> p a n", a=4),
                    DEN.rearrange("p (a n) -> p a n", a=4), ALU.mult), ["ps%d" % ob, "DENr"], ["bT"])
        for hi, tl in enumerate(((14, 15, 0), (7, 8, 9))):
            ob, sb = 4 + hi, 6 + hi
            for ti, ktg in enumerate(tl):
                b = psbank()
                mm_one(ps[:, b, 0:4].rearrange("p (a n) -> p a n", a=4), wkT_g[:, ktg * 128:(ktg + 1) * 128],
                       wqT_g[:, :, 1024 + hi:1025 + hi], True, True, ["wk%d_%d" % (g, ktg // 4)] + WQK, "ps%d" % b)
                dve(lambda e, b=b, hi=hi, ti=ti: e.scalar_tensor_tensor(
                    tmph, ps[:, b, 0:4], SC_W, hbw_g[:, hi, ti, :], ALU.mult, ALU.add),
                    ["ps%d" % b, "wbias"], ["tmph"])
                act(Ph, tmph, AF.Exp, ["tmph"], ["Ph"])

                def pvh(eng, _s, ti=ti, ktg=ktg, ob=ob, sb=sb, wv_g=wv_g):
                    eng.matmul(ps[:, ob, 0:4], wv_g[:, ktg, :], Ph, start=(ti == 0), stop=(ti == 2))
                    return eng.matmul(ps[:, sb, 0:4], ones_b, Ph, start=(ti == 0), stop=(ti == 2))
                sc.op("pe", pvh, reads=["Ph", "wv%d" % ktg, "ones_b"], writes=["ps%d" % ob, "ps%d" % sb])
            dve(lambda e, sb=sb, g=g: e.tensor_tensor(DENh, ps[:, sb, 0:4], esink[:, 4 * g:4 * g + 4], ALU.add),
                ["ps%d" % sb, "esink"], ["DENh"])
            dve(lambda e: e.reciprocal(DENh, DENh), ["DENh"], ["DENh"])
            dve(lambda e, ob=ob, hi=hi, g=g: e.tensor_tensor(
                bT[:, 4 * g:4 * g + 4, 1024 + hi:1025 + hi], ps[:, ob, 0:4].rearrange("p (a n) -> p a n", a=4),
                DENh.rearrange("p (a n) -> p a n", a=4), ALU.mult), ["ps%d" % ob, "DENh"], ["bT"])
    sc.barrier()
    if stop == 'C':
        return finish_dbg()

    memb = bfv(8224, 4096).rearrange("p (a n) -> p a n", a=2)
    memT = bfv(10272, 4096).rearrange("p (c n) -> p c n", c=16)
    mkT = bfv(12320, 2048).rearrange("p (c n) -> p c n", c=8)
    mv = bfv(13344, 2048).rearrange("p (a n) -> p a n", a=2)
    mqT = [bfv(28728, 2 * NQ).rearrange("p (a n) -> p a n", a=2), bfv(29754, 2 * NQ).rearrange("p (a n) -> p a n", a=2)]
    Pm = [bfv(30780, 684).rearrange("p (a n) -> p a n", a=2), bfv(31122, 684).rearrange("p (a n) -> p a n", a=2)]
    RM = f32v(31464, 342)
    SC_M = 256 ** -0.5

    def fnm(e, sem):
        i = e.dma_start(out=memb, in_=mem.rearrange("(a p) n -> p a n", p=128))
        i.then_inc(sem, 16)
        return i
    sc.op("pool", fnm, writes=["memb"], dma=("memb", 1))
    for mt in range(2):
        for g4 in range(4):
            b = psbank()
            pb = ps[:, b, :].bitcast(BF16)
            tr_group([(pb[:, j * 128:(j + 1) * 128], memb[:, mt, (g4 * 4 + j) * 128:(g4 * 4 + j + 1) * 128])
                      for j in range(4)], ident_b, ["memb", "ident_b"], "ps%d" % b)
            evac(memT[:, g4 * 4:g4 * 4 + 4, mt * 128:(mt + 1) * 128],
                 pb[:, 0:512].rearrange("p (a n) -> p a n", a=4), ["ps%d" % b], ["memT%d_%d" % (mt, g4)])
    MEMTK = ["memT%d_%d" % (a, c) for a in range(2) for c in range(4)]
    for sp_ in range(4):
        si = load_w([(w_mkv[:, sp_ * 256:(sp_ + 1) * 256], 0)])
        sl = wslot(si)
        for j in range(2):
            b = psbank()
            mm_group(ps[:, b, 0:256], [(sl[:, kc, j * 128:(j + 1) * 128], memT[:, kc, :]) for kc in range(KC)],
                     MEMTK + ["w%d" % si], "ps%d" % b)
            evac(mkT[:, 2 * sp_ + j, :], ps[:, b, 0:256], ["ps%d" % b], ["mkT%d" % (2 * sp_ + j)])
    for sp_ in range(4):
        si = load_w([(w_mkv[:, 1024 + sp_ * 256:1024 + (sp_ + 1) * 256], 0)])
        sl = wslot(si)
        for mt in range(2):
            b = psbank()
            mm_group(ps[:, b, 0:256], [(memT[:, kc, mt * 128:(mt + 1) * 128], sl[:, kc, 0:256]) for kc in range(KC)],
                     MEMTK + ["w%d" % si], "ps%d" % b)
            evac(mv[:, mt, sp_ * 256:(sp_ + 1) * 256], ps[:, b, 0:256], ["ps%d" % b], ["mv%d_%d" % (mt, sp_)])
    for hm in range(4):
        si = load_w([(w_in[:, 4608 + hm * 256:4608 + (hm + 1) * 256], 0)])
        mq = mqT[hm % 2]
        for dd in range(2):
            proj_fm(si, dd * 128, lambda c: hT_own[:, :, QCH[c][0]:QCH[c][0] + 342], 3, 342,
                    lambda c, dd=dd, mq=mq: mq[:, dd, QCH[c][0]:QCH[c][0] + 342], HK_OWN, "mq%d_%d" % (hm % 2, dd))
        for c, (c0, w) in enumerate(QCH):
            pi = (hm * 3 + c) % 2
            for mt in range(2):
                b = psbank()
                mm_group(ps[:, b, 0:w], [(mkT[:, 2 * hm + dd, mt * 128:(mt + 1) * 128], mq[:, dd, c0:c0 + w])
                                         for dd in range(2)],
                         ["mkT%d" % (2 * hm), "mkT%d" % (2 * hm + 1), "mq%d_0_%d" % (hm % 2, c),
                          "mq%d_1_%d" % (hm % 2, c)], "ps%d" % b)
                act(Pm[pi][:, mt, 0:w], ps[:, b, 0:w], AF.Exp, ["ps%d" % b], ["Pm%d_%d" % (pi, mt)], scale=SC_M)
            PK = ["Pm%d_0" % pi, "Pm%d_1" % pi]
            for dvc in range(2):
                mm_group(ps[:, 4 + dvc, 0:w], [(mv[:, mt, hm * 256 + dvc * 128:hm * 256 + (dvc + 1) * 128],
                                                Pm[pi][:, mt, 0:w]) for mt in range(2)],
                         PK + ["mv%d_%d" % (mt, hm) for mt in range(2)], "ps%d" % (4 + dvc))
            mm_group(ps[:, 6, 0:w], [(ones_b, Pm[pi][:, mt, 0:w]) for mt in range(2)], PK + ["ones_b"], "ps6")
            dve(lambda e, w=w: e.reciprocal(RM[:, 0:w], ps[:, 6, 0:w]), ["ps6"], ["RM"])
            for dvc in range(2):
                dve(lambda e, dvc=dvc, hm=hm, c0=c0, w=w: e.tensor_tensor(
                    cT[:, 2 * hm + dvc, c0:c0 + w], ps[:, 4 + dvc, 0:w], RM[:, 0:w], ALU.mult),
                    ["ps%d" % (4 + dvc), "RM"], ["cT"])
    sc.barrier()
    if stop == 'D':
        return finish_dbg()

    rr["psmod"] = 8
    gT = bfv(32832, 16 * NQ).rearrange("p (c n) -> p c n", c=16)
    SG = [f32v(8224, NQ), f32v(9250, NQ)]
    GAs = [f32v(10276, NQ), f32v(11302, NQ)]
    TP = f32v(12328, NQ)
    brT = [aT, bT, cT]
    BRK = ["aT", "bT", "cT"]

    def v3(ap):
        return ap.rearrange("p (a n) -> p a n", a=3)
    sgi = 0
    for dp in range(8):
        for n in range(3):
            sg_ = load_w([(w_gate[:, n * 2048 + dp * 256:n * 2048 + (dp + 1) * 256], 0)])
            sb_ = load_w([(w_br[n, :, dp * 256:(dp + 1) * 256], 0)])
            slg, slb = wslot(sg_), wslot(sb_)
            for j in range(2):
                dc = 2 * dp + j

                def fng(eng, _s, slg=slg, j=j):
                    inst = None
                    for c, (c0, w) in enumerate(QCH):
                        for kc in range(KC):
                            inst = eng.matmul(ps[:, c, 0:w], slg[:, kc, j * 128:(j + 1) * 128], hT_own[:, kc, c0:c0 + w],
                                              start=(kc == 0), stop=(kc == KC - 1))
                    return inst
                sc.op("pe", fng, reads=HK_OWN + ["w%d" % sg_], writes=["ps0", "ps1", "ps2"])

                def fnw(eng, _s, slb=slb, j=j, n=n):
                    inst = None
                    for c, (c0, w) in enumerate(QCH):
                        for kc in range(8):
                            inst = eng.matmul(ps[:, 3 + c, 0:w], slb[:, kc, j * 128:(j + 1) * 128],
                                              brT[n][:, kc, c0:c0 + w], start=(kc == 0), stop=(kc == 7))
                    return inst
                sc.op("pe", fnw, reads=[BRK[n], "w%d" % sb_], writes=["ps3", "ps4", "ps5"])
                sgb = SG[sgi % 2]; sk = "SG%d" % (sgi % 2); sgi += 1
                act(v3(sgb), ps[:, 0:3, 0:342], AF.Sigmoid, ["ps0", "ps1", "ps2", "pp"], [sk],
                    bias=ppc(PP_BG + n * 16 + dc))
                GA = GAs[j]; gk = "GA%d" % j
                if n == 0:
                    dve(lambda e, sgb=sgb, GA=GA: e.tensor_tensor(v3(GA), v3(sgb), ps[:, 3:6, 0:342], ALU.mult),
                        [sk, "ps3", "ps4", "ps5"], [gk])
                else:
                    dve(lambda e, sgb=sgb: e.tensor_tensor(v3(TP), v3(sgb), ps[:, 3:6, 0:342], ALU.mult),
                        [sk, "ps3", "ps4", "ps5"], ["TP"])
                    if n == 1:
                        dve(lambda e, GA=GA: e.tensor_tensor(GA, GA, TP, ALU.add), [gk, "TP"], [gk])
                    else:
                        dve(lambda e, dc=dc, GA=GA: e.tensor_tensor(gT[:, dc, :], GA, TP, ALU.add), [gk, "TP"],
                            ["gT%d" % dc])
    sc.barrier()
    if stop == 'E':
        return finish_dbg()

    vT = f32v(16416, 16 * NQ).rearrange("p (c n) -> p c n", c=16)
    h1T = hT_own
    xtF = [f32v(8224, 2048), f32v(10272, 2048)]
    GTK = ["gT%d" % dc for dc in range(16)]
    def f_stage1(t):
        sl = t % 2
        xk = "xtF%d" % sl
        if t < 8:
            dma_in(xtF[sl], xs[t * 128:(t + 1) * 128, :], xk, xk)
            st = t
        else:
            dma_in(xtF[sl], xh, xk, xk)
            st = 16
            ln_stats(xtF[sl], 16, xk)
        act(xtF[sl], xtF[sl], AF.Identity, [xk, "sc%d" % st, "scb%d" % st], [xk],
            bias=stat_sc[:, st, 1:2], scale=stat_sc[:, st, 0:1])

    def f_stage2(t):
        sl = t % 2
        xk = "xtF%d" % sl
        for g4 in range(4):
            b = psbank()
            tr_group([(ps[:, b, j * 128:(j + 1) * 128], xtF[sl][:, (g4 * 4 + j) * 128:(g4 * 4 + j + 1) * 128])
                      for j in range(4)], ident_f, [xk, "ident_f"], "ps%d" % b)
            for j in range(4):
                dc = g4 * 4 + j
                if t < 8:
                    dst = vT[:, dc, t * 128:(t + 1) * 128]; src = ps[:, b, j * 128:(j + 1) * 128]
                else:
                    dst = vT[:, dc, 1024:1026]; src = ps[:, b, j * 128:j * 128 + 2]
                if g4 % 2 == 0:
                    act(dst, src, AF.Identity, ["ps%d" % b, "ag0", "ab0"], ["v%d_%d" % (dc, t)],
                        bias=ab0[:, dc:dc + 1], scale=ag0[:, dc:dc + 1])
                else:
                    dve(lambda e, dst=dst, src=src, dc=dc: e.tensor_scalar(
                        dst, src, ag0[:, dc:dc + 1], ab0[:, dc:dc + 1], ALU.mult, ALU.add),
                        ["ps%d" % b, "ag0", "ab0"], ["v%d_%d" % (dc, t)])
    f_stage1(0)
    for t in range(9):
        if t + 1 < 9:
            f_stage1(t + 1)
        f_stage2(t)
    for dp in range(8):
        si = load_w([(w_o[:, dp * 256:(dp + 1) * 256], 0)])
        sl_ = wslot(si)
        for j in range(2):
            dc = 2 * dp + j
            b0 = 0 if (dc % 2 == 0) else 3

            def fno(eng, _s, sl_=sl_, j=j, b0=b0):
                inst = None
                for c, (c0, w) in enumerate(QCH):
                    for kc in range(KC):
                        inst = eng.matmul(ps[:, b0 + c, 0:w], sl_[:, kc, j * 128:(j + 1) * 128], gT[:, kc, c0:c0 + w],
                                          start=(kc == 0), stop=(kc == KC - 1))
                return inst
            pk = ["ps%d" % (b0 + c) for c in range(3)]
            sc.op("pe", fno, reads=GTK + ["w%d" % si], writes=pk)
            dve(lambda e, dc=dc, b0=b0: e.tensor_tensor(v3(vT[:, dc, :]), v3(vT[:, dc, :]), ps[:, b0:b0 + 3, 0:342],
                                                       ALU.add), pk + ["v%d_%d" % (dc, t) for t in range(9)],
                ["v%d" % dc])
    sc.barrier()
    if stop == 'F2':
        return finish_dbg()

    def layer_norm_fm(V, ncol, gcolf, bcolf, post, tmpbase, nchunks, cw):
        S1 = f32v(tmpbase, ncol); S2 = f32v(tmpbase + 1026, ncol)
        SQ = [f32v(tmpbase + 2052, ncol), f32v(tmpbase + 3078, ncol)]

        def vc(ap):
            return ap.rearrange("p (a n) -> p a n", a=nchunks)
        for dc in range(16):
            vv = V[:, dc, 0:ncol]
            if dc == 0:
                dve(lambda e, vv=vv: e.tensor_copy(S1, vv), ["v0"], ["S1"])
            else:
                dve(lambda e, vv=vv: e.tensor_tensor(S1, S1, vv, ALU.add), ["v%d" % dc, "S1"], ["S1"])
            q = SQ[dc % 2]
            act(q, vv, AF.Square, ["v%d" % dc], ["SQ%d" % (dc % 2)])
            if dc == 0:
                dve(lambda e, q=q: e.tensor_copy(S2, q), ["SQ0"], ["S2"])
            else:
                dve(lambda e, q=q: e.tensor_tensor(S2, S2, q, ALU.add), ["SQ%d" % (dc % 2), "S2"], ["S2"])
        for c in range(nchunks):
            mm_one(ps[:, c, 0:cw], ones_f, S1[:, c * cw:(c + 1) * cw], True, True, ["S1", "ones_f"], "ps%d" % c)
            mm_one(ps[:, 4 + c, 0:cw], ones_f, S2[:, c * cw:(c + 1) * cw], True, True, ["S2", "ones_f"],
                   "ps%d" % (4 + c))
        pa = ["ps%d" % c for c in range(nchunks)]
        pb_ = ["ps%d" % (4 + c) for c in range(nchunks)]
        dve(lambda e: e.tensor_scalar(vc(S1), ps[:, 0:nchunks, 0:cw], 1.0 / D, None, ALU.mult), pa, ["S1"])
        dve(lambda e: e.tensor_tensor(SQ[0], S1, S1, ALU.mult), ["S1"], ["SQ0"])
        dve(lambda e: e.scalar_tensor_tensor(vc(S2), ps[:, 4:4 + nchunks, 0:cw], 1.0 / D, vc(SQ[0]), ALU.mult,
                                             ALU.subtract), pb_ + ["SQ0"], ["S2"])
        act(S2, S2, AF.Ln, ["S2", "epsc"], ["S2"], bias=epsc)
        act(S2, S2, AF.Exp, ["S2"], ["S2"], scale=-0.5)
        for dc in range(16):
            vv = V[:, dc, 0:ncol]
            dve(lambda e, vv=vv: e.tensor_tensor(vv, vv, S1, ALU.subtract), ["v%d" % dc, "S1"], ["v%d" % dc])
            dve(lambda e, vv=vv: e.tensor_tensor(vv, vv, S2, ALU.mult), ["v%d" % dc, "S2"], ["v%d" % dc])
            post(dc, vv)

    def post1(dc, vv):
        act(h1T[:, dc, 0:NQ], vv, AF.Identity, ["v%d" % dc, "pp"], ["h1T%d" % dc],
            bias=ppc(PP_LN + 48 + dc), scale=ppc(PP_LN + 32 + dc))
        dve(lambda e: e.tensor_scalar(vv, vv, ag1[:, dc:dc + 1], ab1[:, dc:dc + 1], ALU.mult, ALU.add),
            ["v%d" % dc, "ag1", "ab1"], ["v%d" % dc])
    layer_norm_fm(vT, NQ, None, None, post1, 8224, 3, 342)
    sc.barrier()
    if stop == 'F':
        return finish_dbg()

    H1K = ["h1T%d" % dc for dc in range(16)]
    actT = [bfv(8224, 8192).rearrange("p (c n) -> p c n", c=8), bfv(12320, 8192).rearrange("p (c n) -> p c n", c=8)]
    UV = [f32v(32832, NQ), f32v(33858, NQ)]
    UG = [f32v(34884, NQ), f32v(35910, NQ)]
    CT = [f32v(36936, 1024), f32v(37960, 1024)]
    GL = f32v(38984, 1024)
    groups = [list(range(a, min(a + 8, NJ))) for a in range(0, NJ, 8)]
    prc = 0
    prcc = {"n": 0}

    def ffn_up(gi, grp):
        ab_ = gi % 2
        for jj, j in enumerate(grp):
            si = load_w([(w_up[:, j * 128:(j + 1) * 128], 0), (w_up[:, D_FF + j * 128:D_FF + (j + 1) * 128], 128)])
            sl_ = wslot(si)
            u = (gi * 8 + jj) % 2
            for which, (U, b0, coff, chunk) in enumerate(((UV[u], 0, 0, j), (UG[u], 3, 128, NJ + j))):
                def fnu(eng, _s, sl_=sl_, b0=b0, coff=coff):
                    inst = None
                    for c, (c0, w) in enumerate(QCH):
                        for kc in range(KC):
                            inst = eng.matmul(ps[:, b0 + c, 0:w], sl_[:, kc, coff:coff + 128], h1T[:, kc, c0:c0 + w],
                                              start=(kc == 0), stop=(kc == KC - 1))
                    return inst
                pk = ["ps%d" % (b0 + c) for c in range(3)]
                sc.op("pe", fnu, reads=H1K + ["w%d" % si], writes=pk)
                uk = "U%d_%d" % (which, u)
                act(U[:, 1:685].rearrange("p (a n) -> p a n", a=2), ps[:, b0:b0 + 2, 0:342], AF.Copy, pk[0:2], [uk + "a"])
                act(U[:, 685:1025], ps[:, b0 + 2, 0:340], AF.Copy, [pk[2]], [uk + "b"])
                act(U[:, 0:1], ps[:, b0 + 2, 340:341], AF.Identity, [pk[2], "pp"], [uk + "c"], scale=ppc(PP_FLAG))
                act(U[:, 1025:1026], ps[:, b0 + 2, 341:342], AF.Identity, [pk[2], "pp"], [uk + "d"],
                    scale=ppc(PP_FLAG + 1))
                ukeys = [uk + x for x in "abcd"]
                ct = CT[which]
                ck = "CT%d" % which
                dve(lambda e, U=U, ct=ct, chunk=chunk: e.tensor_scalar(ct, U[:, 0:1024], ppc(PP_CW + chunk), None,
                                                                       ALU.mult), ukeys + ["pp"], [ck])
                for k in (1, 2):
                    dve(lambda e, U=U, ct=ct, chunk=chunk, k=k: e.scalar_tensor_tensor(
                        ct, U[:, k:k + 1024], ppc(PP_CW + 86 * k + chunk), ct, ALU.mult, ALU.add),
                        ukeys + [ck, "pp"], [ck])
            act(GL, CT[1], AF.Gelu_apprx_tanh, ["CT1", "pp"], ["GL"], bias=ppc(PP_CB + NJ + j))
            dve(lambda e, ab_=ab_, jj=jj, j=j: e.scalar_tensor_tensor(
                actT[ab_][:, jj, :], CT[0], ppc(PP_CB + j), GL, ALU.add, ALU.mult), ["CT0", "GL", "pp"],
                ["actT%d_%d" % (ab_, jj)])

    def ffn_down(gi, grp):
        ab_ = gi % 2
        nj = len(grp)
        j0 = grp[0]
        AK = ["actT%d_%d" % (ab_, jj) for jj in range(nj)]
        for dp in range(8):
            si = load_w([(w_down[j0 * 128:(j0 + nj) * 128, dp * 256:(dp + 1) * 256], 0)])
            sl_ = wslot(si)
            for jd in range(2):
                dc = 2 * dp + jd
                b0 = (6, 0, 2, 4)[prcc["n"] % 4]; prcc["n"] += 1

                def fnd(eng, _s, sl_=sl_, jd=jd, b0=b0, nj=nj, ab_=ab_):
                    inst = None
                    for hf in range(2):
                        for jj in range(nj):
                            inst = eng.matmul(ps[:, b0 + hf, :], sl_[:, jj, jd * 128:(jd + 1) * 128],
                                              actT[ab_][:, jj, hf * 512:(hf + 1) * 512], start=(jj == 0),
                                              stop=(jj == nj - 1))
                    return inst
                pk = ["ps%d" % b0, "ps%d" % (b0 + 1)]
                sc.op("pe", fnd, reads=AK + ["w%d" % si], writes=pk)
                dve(lambda e, dc=dc, b0=b0: e.tensor_tensor(
                    vT[:, dc, 0:1024].rearrange("p (a n) -> p a n", a=2),
                    vT[:, dc, 0:1024].rearrange("p (a n) -> p a n", a=2), ps[:, b0:b0 + 2, :], ALU.add),
                    pk + ["v%d" % dc], ["v%d" % dc])

    ffn_up(0, groups[0])
    for gi in range(1, len(groups)):
        ffn_up(gi, groups[gi])
        ffn_down(gi - 1, groups[gi - 1])
    ffn_down(len(groups) - 1, groups[-1])
    sc.barrier()
    if stop == 'G':
        return finish_dbg()

    def post2(dc, vv):
        dve(lambda e: e.tensor_scalar(vv, vv, ppc(PP_LN + 64 + dc), ppc(PP_LN + 80 + dc), ALU.mult, ALU.add),
            ["v%d" % dc, "pp"], ["v%d" % dc])
    layer_norm_fm(vT, 1024, None, None, post2, 32832, 2, 512)
    OUTT = [f32v(36936, 2048), f32v(38984, 2048)]
    VK = ["v%d" % dc for dc in range(16)]
    for t in range(8):
        o = OUTT[t % 2]
        ok = "OUT%d" % (t % 2)
        for g4 in range(4):
            b = psbank()
            tr_group([(ps[:, b, j * 128:(j + 1) * 128], vT[:, g4 * 4 + j, t * 128:(t + 1) * 128]) for j in range(4)],
                     ident_f, VK + ["ident_f"], "ps%d" % b)
            evac(o[:, g4 * 512:(g4 + 1) * 512], ps[:, b, :], ["ps%d" % b], [ok + "_%d" % g4])

        def fny(e, sem, o=o, t=t):
            i = e.dma_start(out=y[t * 128:(t + 1) * 128, :], in_=o)
            i.then_inc(sem, 16)
            return i
        sc.op("sp", fny, reads=[ok + "_%d" % g4 for g4 in range(4)], writes=["y%d" % t], dma=(ok, 1))
    sc.op("sp", lambda e, _s: e.nop(), reads=["y%d" % t for t in range(8)], writes=["done"])
    sc.emit(nc, stack, block)
    stack.close()
    nc._sched = sc
    return nc


def _t5_bucket(rel):
    half, me = 16, 8
    rel = np.asarray(rel, dtype=np.int64)
    ret = np.where(rel > 0, half, 0)
    n = np.abs(rel)
    nf = np.maximum(n, 1).astype(np.float32)
    large = me + (np.log(nf / np.float32(me)) / np.float32(math.log(128 / 8)) * np.float32(half - me)).astype(np.int32)
    large = np.minimum(large, half - 1)
    return ret + np.where(n < me, n, large)


_NC_CACHE = {}


def _prep(inp):
    f = lambda k: np.ascontiguousarray(np.asarray(inp[k], dtype=np.float32))
    x = f("x"); memx = f("mem"); table = f("rel_table")
    w_in = f("w_in")[0]; w_mkv = f("w_mem_kv")[0]; w_gate = f("w_gate")[0]; w_br = f("w_branch")[0]
    w_o = f("w_o")[0]; w_up = f("w_up")[0]; w_down = f("w_down")[0]
    i128 = np.arange(128)

    def cols16(v):
        return np.asarray(v, np.float32).reshape(16, 128).T

    pp0 = np.zeros((128, PP_N), np.float32)
    for n_, k in enumerate(("ln_in_g", "ln_in_b")):
        pp0[:, PP_LN + 16 * n_:PP_LN + 16 * (n_ + 1)] = cols16(f(k))
    for n_, k in enumerate(("ln1_g", "ln1_b", "ln2_g", "ln2_b")):
        pp0[:, PP_LN + 32 + 16 * n_:PP_LN + 32 + 16 * (n_ + 1)] = cols16(f(k)[0])
    pp0[:, PP_BG:PP_BG + 48] = f("b_gate")[0].reshape(48, 128).T
    cw = f("conv_w")[0]
    for k in range(3):
        pp0[:, PP_CW + 86 * k:PP_CW + 86 * (k + 1)] = cw[k].reshape(86, 128).T
    pp0[:, PP_CB:PP_CB + 86] = f("conv_b")[0].reshape(86, 128).T
    pp0[:, PP_SG] = f("diff_subln_g")[0]
    for n_, k in enumerate(("diff_lq1", "diff_lk1", "diff_lq2", "diff_lk2")):
        pp0[:, PP_LQ + 64 * n_:PP_LQ + 64 * (n_ + 1)] = f(k)[0][None, :]
    pp0[:, PP_SINK:PP_SINK + 8] = f("win_sink")[0][None, :]
    pp0[:, PP_CHI:PP_CHI + 8] = table[31, 0:8][None, :]
    pp0[:, PP_CLO:PP_CLO + 8] = table[15, 0:8][None, :]

    r = i128[:, None] - np.arange(384)[None, :] + 128
    tz0 = np.transpose(table[_t5_bucket(r)][:, :, 0:8], (0, 2, 1))
    tzd = np.empty((128, 8, 1068), np.float32)
    tzd[:, :, 0:342] = table[31, 0:8][None, :, None]
    tzd[:, :, 342:726] = tz0
    tzd[:, :, 726:1068] = table[15, 0:8][None, :, None]
    tzw = np.zeros((128, 2, 3, 512), np.float32)
    for d in (-1, 0, 1):
        rel = 128 * d + i128[:, None] - i128[None, :]
        bk = _t5_bucket(rel)
        valid = np.abs(rel) <= 128
        for g in range(2):
            for gh in range(4):
                tzw[:, g, d + 1, gh * 128:(gh + 1) * 128] = np.where(valid, table[bk, 8 + 4 * g + gh], MASKV)
    cm = np.eye(128, dtype=np.float32)

    in_maps = []
    for c in range(8):
        b, half = c // 2, c % 2
        own0, oth0 = half * 1024, (1 - half) * 1024
        xo, xt_ = x[b, own0:own0 + 1024], x[b, oth0:oth0 + 1024]
        xs = np.ascontiguousarray(np.concatenate([xo, xt_], axis=0))
        xh = np.ascontiguousarray(np.concatenate([xt_[1023:1024], xt_[0:127]], axis=0))
        pp = pp0.copy()
        pp[:, PP_COT:PP_COT + 8] = table[31 if half == 0 else 15, 0:8][None, :]
        pp[:, PP_FLAG] = 1.0 if half == 1 else 0.0
        pp[:, PP_FLAG + 1] = 1.0 if half == 0 else 0.0

        def kpos(ktg):
            return (own0 + ktg * 128 if ktg < 8 else oth0 + (ktg - 8) * 128) + i128
        cnd = np.zeros((128, 8, 2, 128), np.float32)
        for e, (ktg, qt) in enumerate(((8, 7), (15, 0))):
            rel = kpos(ktg)[:, None] - (own0 + qt * 128 + i128)[None, :]
            cnd[:, :, e, :] = np.transpose(table[_t5_bucket(rel)][:, :, 0:8], (0, 2, 1))
        hq = (own0 - 1, own0 + 1024)
        hbd = np.zeros((128, 8, 16, 2), np.float32)
        for ktg in range(16):
            for jj in range(2):
                rel = kpos(ktg) - hq[jj]
                hbd[:, :, ktg, jj] = table[_t5_bucket(rel)][:, 0:8]
        tze = np.zeros((128, 2, 2, 512), np.float32)
        for e, (ktg, qt) in enumerate(((15, 0), (8, 7))):
            rel = kpos(ktg)[:, None] - (own0 + qt * 128 + i128)[None, :]
            bk = _t5_bucket(rel)
            valid = np.abs(rel) <= 128
            for g in range(2):
                for gh in range(4):
                    tze[:, g, e, gh * 128:(gh + 1) * 128] = np.where(valid, table[bk, 8 + 4 * g + gh], MASKV)
        hbw = np.zeros((128, 2, 2, 3, 4), np.float32)
        for hi, tl in enumerate(((14, 15, 0), (7, 8, 9))):
            if not (0 <= hq[hi] < 2048):
                continue
            for ti, ktg in enumerate(tl):
                rel = kpos(ktg) - hq[hi]
                bk = _t5_bucket(rel)
                valid = np.abs(rel) <= 128
                for g in range(2):
                    for gh in range(4):
                        hbw[:, g, hi, ti, gh] = np.where(valid, table[bk, 8 + 4 * g + gh], MASKV)
        in_maps.append({
            "xs": xs, "xh": xh, "mem": np.ascontiguousarray(memx[b]), "w_in": w_in, "w_mkv": w_mkv,
            "w_gate": w_gate, "w_br": w_br, "w_o": w_o, "w_up": w_up, "w_down": w_down, "pp": pp,
            "tzd": tzd, "cnd": cnd, "hbd": hbd, "tzw": tzw, "tze": tze, "hbw": hbw, "cm": cm,
        })
    return in_maps


def kernel(**inp):
    in_maps = _prep(inp)
    if "nc" not in _NC_CACHE:
        _NC_CACHE["nc"] = build_program()
    res = run_bass_kernel_spmd(_NC_CACHE["nc"], in_maps, core_ids=list(range(8)))
    out = np.zeros((4, 2048, 2048), np.float32)
    for c in range(8):
        b, half = c // 2, c % 2
        out[b, half * 1024:(half + 1) * 1024] = np.asarray(res.results[c]["y"], dtype=np.float32)
    return out
```

```python
import math
from contextlib import ExitStack
import numpy as np
import concourse.bass as bass
import concourse.mybir as mybir
from concourse.bass_utils import run_bass_kernel_spmd

F32 = mybir.dt.float32
BF16 = mybir.dt.bfloat16
AF = mybir.ActivationFunctionType
ALU = mybir.AluOpType
AX = mybir.AxisListType

D = 2048
KC = 16
S = 2048
T = 1024
NQ = 1026
OTH = 1028
HTW = 2052
IN_W = 5632
D_FF = 5504
NJ = 43
ALPHA = 2 ** 0.25
LN_EPS = 1e-5
LAMBDA_INIT = 0.2
QCH = [(0, 342), (342, 342), (684, 342)]
WCOLS = 256
NSLOT = 4
MASKV = -200.0

PP_LN = 0
PP_BG = 96
PP_CW = 144
PP_CB = 402
PP_SG = 488
PP_LQ = 489
PP_SINK = 745
PP_CHI = 753
PP_CLO = 761
PP_COT = 769
PP_FLAG = 777
PP_N = 780


import os
DBGCUT = int(os.environ.get('DBGCUT', '0'))


class _Cut(Exception):
    pass


class Sched:
    SEM_LIMIT = 20000

    def __init__(self):
        self.ops = []
        self.last_writer = {}
        self.readers = {}
        self.floor = []
        self.dma_count = {}

    def op(self, eng, fn, reads=(), writes=(), dma=None, nobar=False):
        i = len(self.ops)
        deps = set()
        for k in reads:
            w = self.last_writer.get(k)
            if w is not None:
                deps.add(w)
            if k.startswith("ps"):
                for r in self.readers.get(k, ()):
                    if self.ops[r]["eng"] != eng:
                        deps.add(r)
        for k in writes:
            w = self.last_writer.get(k)
            if w is not None:
                deps.add(w)
            for r in self.readers.get(k, ()):
                deps.add(r)
        if not nobar:
            deps.update(self.floor)
        deps.discard(i)
        o = dict(eng=eng, fn=fn, deps=deps, sig=False, dma=dma)
        if dma is not None:
            semkey, n = dma
            c = self.dma_count.get(semkey, 0) + n
            self.dma_count[semkey] = c
            o["dval"] = 16 * c
        self.ops.append(o)
        for k in reads:
            self.readers.setdefault(k, []).append(i)
        for k in writes:
            self.last_writer[k] = i
            self.readers[k] = []
        return i

    def barrier(self):
        self.marks = getattr(self, "marks", [])
        self.marks.append(len(self.ops))
        last = {}
        for i, o in enumerate(self.ops):
            last[o["eng"]] = i
            if o["dma"] is not None:
                last[("dma", o["dma"][0])] = i
        self.floor = list(last.values())

    def emit(self, nc, stack, block):
        ops = self.ops
        for o in ops:
            for d in o["deps"]:
                po = ops[d]
                if po["dma"] is None:
                    if po["eng"] == "pe" and o["eng"] == "pe":
                        continue
                    po["sig"] = True
        cnt = {}
        for o in ops:
            if o["dma"] is None and o["sig"]:
                e = o["eng"]
                c = cnt.get(e, 0)
                o["sv"] = (e, c // self.SEM_LIMIT, c % self.SEM_LIMIT + 1)
                cnt[e] = c + 1
        self.mark_pe = []
        for mk in getattr(self, "marks", []):
            self.mark_pe.append(sum(1 for o in ops[:mk] if o["dma"] is None and o["sig"] and o["eng"] == "pe"))
        sems = {}

        def getsem(key):
            if key not in sems:
                sems[key] = self.sem_pool[len(sems)]
            return sems[key]
        for e, c in cnt.items():
            for ep in range((c + self.SEM_LIMIT - 1) // self.SEM_LIMIT):
                getsem((e, ep))
        for k in self.dma_count:
            getsem(("dma", k))
        progs = {}
        waited = {}
        for i, o in enumerate(ops):
            e = o["eng"]
            wl = {}
            for d in sorted(o["deps"]):
                po = ops[d]
                if po["dma"] is not None:
                    key = ("dma", po["dma"][0]); val = po["dval"]
                    order = (0, val)
                else:
                    if po["eng"] == "pe" and e == "pe":
                        continue
                    pe_, ep, v = po["sv"]
                    key = (pe_, ep); val = v
                    order = (ep, v)
                stream = key if key[0] == "dma" else key[0]
                w = waited.setdefault(e, {})
                if stream in w and w[stream] >= order:
                    continue
                w[stream] = order
                wl[stream] = (key, val)
            progs.setdefault(e, []).append((list(wl.values()), o))
        self.n_sems = len(sems)

        self.trace = {e: [([ (k, v) for k, v in w], o.get("sv"), o.get("dval"), o["dma"]) for w, o in items] for e, items in progs.items()}

        def run(eng, items):
            for waits, o in items:
                for key, val in waits:
                    eng.wait_ge(sems[key], val)
                inst = o["fn"](eng, sems[("dma", o["dma"][0])] if o["dma"] is not None else None)
                if o["dma"] is None and o["sig"]:
                    e_, ep, v = o["sv"]
                    inst.then_inc(sems[(e_, ep)], 1)

        if "pe" in progs:
            @block.tensor
            def _(t):
                run(t, progs["pe"])
        if "act" in progs:
            @block.scalar
            def _(a):
                run(a, progs["act"])
        if "dve" in progs:
            @block.vector
            def _(v):
                run(v, progs["dve"])
        if "pool" in progs:
            @block.gpsimd
            def _(g):
                run(g, progs["pool"])
        if "sp" in progs:
            @block.sync
            def _(s):
                run(s, progs["sp"])


def build_program(stop=None, dbg_off=0, dbg_n=2048):
    nc = bass.Bass("TRN2", target_bir_lowering=False)

    def din(name, shape):
        return nc.dram_tensor(name, list(shape), F32, kind="ExternalInput").ap()
    xs = din("xs", [S, D])
    xh = din("xh", [128, D])
    mem = din("mem", [256, D])
    w_in = din("w_in", [D, IN_W])
    w_mkv = din("w_mkv", [D, D])
    w_gate = din("w_gate", [D, 3 * D])
    w_br = din("w_br", [3, 1024, D])
    w_o = din("w_o", [D, D])
    w_up = din("w_up", [D, 2 * D_FF])
    w_down = din("w_down", [D_FF, D])
    pp_d = din("pp", [128, PP_N])
    tzd_d = din("tzd", [128, 8, 1068])
    cnd_d = din("cnd", [128, 8, 2, 128])
    hbd_d = din("hbd", [128, 8, 16, 2])
    tzw_d = din("tzw", [128, 2, 3, 512])
    tze_d = din("tze", [128, 2, 2, 512])
    hbw_d = din("hbw", [128, 2, 2, 3, 4])
    y = nc.dram_tensor("y", [T, D], F32, kind="ExternalOutput").ap()
    dbg = nc.dram_tensor("dbg", [128, dbg_n], F32, kind="ExternalOutput").ap() if stop else None

    stack = ExitStack()
    XW = 43164
    big = stack.enter_context(nc.sbuf_tensor("big", [128, XW], F32))
    wr = stack.enter_context(nc.sbuf_tensor("wr", [128, NSLOT * KC * WCOLS], BF16))
    cst = stack.enter_context(nc.sbuf_tensor("cst", [128, 1536], F32))
    ps = stack.enter_context(nc.psum_tensor("ps", [128, 8, 512], F32))
    sem_pool = [stack.enter_context(nc.semaphore("s%d" % i)) for i in range(40)]
    block = stack.enter_context(nc.Block())
    sc = Sched()
    sc.sem_pool = sem_pool

    def f32v(off, n):
        return big[:, off:off + n]

    def bfv(off, nbf):
        assert nbf % 2 == 0
        return big[:, off:off + nbf // 2].bitcast(BF16)

    pp = cst[:, 0:PP_N]
    ident_f = cst[:, 800:928]
    ones_avg = cst[:, 928:1056]
    ones_f = cst[:, 1056:1184]
    ident_b = cst[:, 1184:1248].bitcast(BF16)
    ones_b = cst[:, 1248:1312].bitcast(BF16)
    misc = cst[:, 1312:1536]
    neglam = misc[:, 0:1]
    gcol = misc[:, 1:2]
    epsc = misc[:, 2:3]
    esink = misc[:, 3:11]
    lsum = misc[:, 11:13]
    lexp = misc[:, 13:15]
    zcol = misc[:, 15:16]
    stat_all = misc[:, 16:16 + 18 * 2].rearrange("p (t k) -> p t k", k=2)
    stat_sc = misc[:, 52:52 + 18 * 2].rearrange("p (t k) -> p t k", k=2)
    bnst = misc[:, 96:96 + 48].rearrange("p (s k) -> p s k", s=2)
    ag0 = misc[:, 144:160]
    ab0 = misc[:, 160:176]
    ag1 = misc[:, 176:192]
    ab1 = misc[:, 192:208]
    lntmp = misc[:, 208:224]

    def ppc(off, n=1):
        return pp[:, off:off + n]

    def finish_dbg():
        def fnd(e, sem):
            i = e.dma_start(out=dbg, in_=big[:, dbg_off:dbg_off + dbg_n])
            i.then_inc(sem, 16)
            return i
        sc.op("sp", fnd, reads=[], writes=["dbgout"], dma=("dbgout", 1))
        sc.op("sp", lambda e, _s: e.nop(), reads=["dbgout"], writes=["done"])
        sc.emit(nc, stack, block)
        stack.close()
        nc._sched = sc
        return nc

    rr = {"ps": 0, "slot": 0, "psmod": 8}
    slot_uses = [0] * NSLOT

    def wslot(i):
        return wr[:, i * KC * WCOLS:(i + 1) * KC * WCOLS].rearrange("p (c n) -> p c n", c=KC)

    def load_w(pieces):
        si = rr["slot"] % NSLOT
        rr["slot"] += 1
        sl = wslot(si)

        def fn(eng, sem, pieces=pieces, sl=sl):
            inst = None
            for src, off in pieces:
                rows, ncols = src.shape
                nk = rows // 128
                inst = eng.dma_start(out=sl[:, 0:nk, off:off + ncols],
                                     in_=src.rearrange("(c p) n -> p c n", p=128))
                inst.then_inc(sem, 16)
            return inst
        sc.op("pool", fn, writes=["w%d" % si], dma=("w%d" % si, len(pieces)), nobar=True)
        return si

    def dma_in(dst, src, key, semkey, eng="sp"):
        def fn(e, sem, dst=dst, src=src):
            inst = e.dma_start(out=dst, in_=src)
            inst.then_inc(sem, 16)
            return inst
        sc.op(eng, fn, writes=[key], dma=(semkey, 1))

    def mm_group(out, pairs, reads, pskey):
        def fn(eng, _s, out=out, pairs=pairs):
            inst = None
            n = len(pairs)
            for i, (l, r) in enumerate(pairs):
                inst = eng.matmul(out, l, r, start=(i == 0), stop=(i == n - 1))
            return inst
        sc.op("pe", fn, reads=reads, writes=[pskey])

    def mm_one(out, l, r, start, stop, reads, pskey):
        def fn(eng, _s):
            return eng.matmul(out, l, r, start=start, stop=stop)
        sc.op("pe", fn, reads=reads, writes=[pskey])

    def tr_group(outs_ins, ident, reads, pskey):
        def fn(eng, _s):
            inst = None
            for o, i_ in outs_ins:
                inst = eng.transpose(o, i_, ident)
            return inst
        sc.op("pe", fn, reads=reads, writes=[pskey])

    def act(out, in_, func, reads, writes, bias=None, scale=None):
        def fn(eng, _s):
            kw = {}
            if bias is not None:
                kw["bias"] = bias
            if scale is not None:
                kw["scale"] = scale
            return eng.activation(out, in_, func, **kw)
        sc.op("act", fn, reads=reads, writes=writes)

    def dve(fn, reads, writes, eng="dve"):
        sc.op(eng, lambda e, _s: fn(e), reads=reads, writes=writes)

    def copy(out, in_, reads, writes, eng="dve"):
        if eng == "act":
            act(out, in_, AF.Copy, reads, writes)
        else:
            dve(lambda e: e.tensor_copy(out, in_), reads, writes, eng=eng)

    def psbank():
        b = rr["ps"] % rr["psmod"]
        rr["ps"] += 1
        return b

    cm_d = din("cm", [128, 128])
    dma_in(pp, pp_d, "pp", "c0")
    dma_in(ident_f, cm_d, "ident_f", "c1")
    dve(lambda e: e.memset(ones_avg, 1.0 / 128.0), [], ["ones_avg"])
    dve(lambda e: e.memset(ones_f, 1.0), [], ["ones_f"])
    dve(lambda e: e.memset(ones_b, 1.0), [], ["ones_b"])
    dve(lambda e: e.memset(epsc, LN_EPS), [], ["epsc"])
    dve(lambda e: e.tensor_copy(ident_b, ident_f), ["ident_f"], ["ident_b"])
    dve(lambda e: e.tensor_scalar(gcol, ppc(PP_SG), 1.0 - LAMBDA_INIT, None, ALU.mult), ["pp"], ["gcol"])
    t64 = f32v(40900, 64)
    for i in range(2):
        dve(lambda e, i=i: e.tensor_tensor(t64, ppc(PP_LQ + 128 * i, 64), ppc(PP_LQ + 128 * i + 64, 64), ALU.mult),
            ["pp"], ["t64"])
        dve(lambda e, i=i: e.reduce_sum(lsum[:, i:i + 1], t64, AX.X), ["t64"], ["lsum%d" % i])
    act(lexp, lsum, AF.Exp, ["lsum0", "lsum1"], ["lexp"])
    dve(lambda e: e.tensor_tensor(neglam, lexp[:, 1:2], lexp[:, 0:1], ALU.subtract), ["lexp"], ["neglam0"])
    dve(lambda e: e.tensor_scalar(neglam, neglam, -LAMBDA_INIT, None, ALU.add), ["neglam0"], ["neglam"])
    act(esink, ppc(PP_SINK, 8), AF.Exp, ["pp"], ["esink"])
    for dst, off, nm in ((ag0, 0, "ag0"), (ab0, 16, "ab0"), (ag1, 32, "ag1"), (ab1, 48, "ab1")):
        dve(lambda e, dst=dst, off=off: e.tensor_scalar(dst, ppc(PP_LN + off, 16), ALPHA, None, ALU.mult),
            ["pp"], [nm])
    CONSTK = ["pp", "ident_f", "ident_b", "ones_avg", "ones_f", "ones_b", "epsc", "gcol", "neglam", "esink",
              "ag0", "ab0", "ag1", "ab1"]

    hT_own = bfv(0, 16 * 1028).rearrange("p (c n) -> p c n", c=16)
    hT_oth = bfv(8224, 16 * 1024).rearrange("p (c n) -> p c n", c=16)
    def HKT(t):
        return ["hT%d_%d" % (t, dc) for dc in range(16)]
    HK_OWN = [k for t in range(8) for k in HKT(t)] + ["hTh"]
    HK_OTH = [k for t in range(8, 16) for k in HKT(t)]

    def hT_tile(t):
        return hT_own[:, :, t * 128:(t + 1) * 128] if t < 8 else hT_oth[:, :, (t - 8) * 128:(t - 7) * 128]

    def hT_chunk512(tc):
        return hT_own[:, :, tc * 512:(tc + 1) * 512] if tc < 2 else hT_oth[:, :, (tc - 2) * 512:(tc - 1) * 512]

    def ln_stats(xt, t, xkey):
        sl = t % 2
        for s4 in range(4):
            dve(lambda e, s4=s4: e.bn_stats(bnst[:, sl, s4 * 6:(s4 + 1) * 6], xt[:, s4 * 512:(s4 + 1) * 512]),
                [xkey], ["bn%d_%d" % (sl, s4)])
        dve(lambda e: e.bn_aggr(stat_all[:, t, :], bnst[:, sl, :]), ["bn%d_%d" % (sl, s) for s in range(4)],
            ["st%d" % t])
        act(stat_sc[:, t, 0:1], stat_all[:, t, 1:2], AF.Ln, ["st%d" % t, "epsc"], ["sca%d" % t], bias=epsc)
        act(stat_sc[:, t, 0:1], stat_sc[:, t, 0:1], AF.Exp, ["sca%d" % t], ["scb%d" % t], scale=-0.5)
        dve(lambda e: e.scalar_tensor_tensor(stat_sc[:, t, 1:2], stat_all[:, t, 0:1], -1.0, stat_sc[:, t, 0:1],
                                             ALU.mult, ALU.mult), ["st%d" % t, "scb%d" % t], ["sc%d" % t])

    xtA = [f32v(28728, 2048), f32v(30776, 2048)]
    xhA = [bfv(32824, 2048), bfv(33848, 2048)]
    def a_stage1(t):
        sl = t % 2
        dma_in(xtA[sl], xs[t * 128:(t + 1) * 128, :], "xt%d" % sl, "xt%d" % sl)
        ln_stats(xtA[sl], t, "xt%d" % sl)
        act(xhA[sl], xtA[sl], AF.Identity, ["xt%d" % sl, "sc%d" % t, "scb%d" % t], ["xh%d" % sl],
            bias=stat_sc[:, t, 1:2], scale=stat_sc[:, t, 0:1])

    def a_stage2(t):
        sl = t % 2
        for g4 in range(4):
            b = psbank()
            pb = ps[:, b, :].bitcast(BF16)
            tr_group([(pb[:, j * 128:(j + 1) * 128], xhA[sl][:, (g4 * 4 + j) * 128:(g4 * 4 + j + 1) * 128])
                      for j in range(4)], ident_b, ["xh%d" % sl, "ident_b"], "ps%d" % b)
            for j in range(4):
                dc = g4 * 4 + j
                dst = hT_tile(t)[:, dc, :]
                src = pb[:, j * 128:(j + 1) * 128]
                if g4 % 2 == 0:
                    act(dst, src, AF.Identity, ["ps%d" % b, "pp"], ["hT%d_%d" % (t, dc)],
                        bias=ppc(PP_LN + 16 + dc), scale=ppc(PP_LN + dc))
                else:
                    dve(lambda e, dst=dst, src=src, dc=dc: e.tensor_scalar(
                        dst, src, ppc(PP_LN + dc), ppc(PP_LN + 16 + dc), ALU.mult, ALU.add),
                        ["ps%d" % b, "pp"], ["hT%d_%d" % (t, dc)])
    a_stage1(0)
    for t in range(16):
        if t + 1 < 16:
            a_stage1(t + 1)
        a_stage2(t)
    dve(lambda e: e.tensor_copy(hT_own[:, :, 1024:1025], hT_oth[:, :, 1023:1024]), HKT(15), ["hTh"])
    dve(lambda e: e.tensor_copy(hT_own[:, :, 1025:1026], hT_oth[:, :, 0:1]), HKT(8) + ["hTh"], ["hTh"])
    sc.barrier()
    if stop == 'A':
        return finish_dbg()

    aT = bfv(16416, 8 * NQ).rearrange("p (c n) -> p c n", c=8)
    bT = bfv(16416 + 4104, 8 * NQ).rearrange("p (c n) -> p c n", c=8)
    cT = bfv(16416 + 8208, 8 * NQ).rearrange("p (c n) -> p c n", c=8)

    evc = {"n": 0}

    def evac(dst, src, reads, writes, eng=None):
        if eng is None:
            eng = "dve" if evc["n"] % 2 == 0 else "act"
            evc["n"] += 1
        copy(dst, src, reads, writes, eng=eng)

    def proj_fm(si, coff, rhs_of_chunk, nchunks, width, dst_of_chunk, reads, wkey, nk=KC, eng=None):
        sl = wslot(si)
        for c in range(nchunks):
            b = psbank()
            rhs = rhs_of_chunk(c)
            mm_group(ps[:, b, 0:width], [(sl[:, kc, coff:coff + 128], rhs[:, kc, :]) for kc in range(nk)],
                     reads + ["w%d" % si], "ps%d" % b)
            evac(dst_of_chunk(c), ps[:, b, 0:width], ["ps%d" % b], [wkey + "_%d" % c], eng=eng)

    dv_g = bfv(28728, 16 * 512).rearrange("p (c n) -> p c n", c=16)
    dkT = [bfv(32824, 2048), bfv(33848, 2048)]
    dqT = [bfv(34872, 1028)[:, 0:NQ], bfv(35386, 1028)[:, 0:NQ]]
    biasb = [(f32v(35900, 1068), f32v(36968, 256).rearrange("p (a n) -> p a n", a=2),
              f32v(43048, 32).rearrange("p (a n) -> p a n", a=16)),
             (f32v(41724, 1068), f32v(42792, 256).rearrange("p (a n) -> p a n", a=2),
              f32v(43080, 32).rearrange("p (a n) -> p a n", a=16))]
    Pb = [bfv(37244, 684).rearrange("p (a n) -> p a n", a=2), bfv(37586, 684).rearrange("p (a n) -> p a n", a=2)]
    tmpb = [f32v(37928, 684).rearrange("p (a n) -> p a n", a=2), f32v(38612, 684).rearrange("p (a n) -> p a n", a=2)]
    R1 = f32v(39296, 342); R2 = f32v(39638, 342); A_ = f32v(39980, 342); SQd = f32v(40322, 342); RSd = f32v(40664, 342)
    SC_D = 64 ** -0.5
    RR = f32v(41040, 684).rearrange("p (a n) -> p a n", a=2)

    def diff_load(h):
        si = load_w([(w_in[:, h * 128:(h + 1) * 128], 0), (w_in[:, 1024 + h * 128:1024 + (h + 1) * 128], 128)])
        tz, cn, hb = biasb[h % 2]

        def fn(e, sem):
            e.dma_start(out=tz, in_=tzd_d[:, h, :]).then_inc(sem, 16)
            e.dma_start(out=cn, in_=cnd_d[:, h, :, :]).then_inc(sem, 16)
            i3 = e.dma_start(out=hb, in_=hbd_d[:, h, :, :])
            i3.then_inc(sem, 16)
            return i3
        sc.op("sp", fn, writes=["bias%d" % (h % 2)], dma=("bias%d" % (h % 2), 3))
        return si

    def diff_segments(h, ktg, c0, w):
        tz, cn, hb = biasb[h % 2]
        segs = []
        own_hi = min(c0 + w, 1024)

        def add(lo, hi, kind, data):
            lo = max(lo, c0); hi = min(hi, own_hi)
            if hi > lo:
                segs.append((lo - c0, hi - c0, kind, data(lo, hi)))
        if ktg < 8:
            kt = ktg
            lb, hb_ = 128 * (kt - 1), 128 * (kt + 2)
            if lb < own_hi and hb_ > c0:
                m0 = c0 - 128 * kt + 128 + 342
                segs.append((0, own_hi - c0, "tile", tz[:, m0:m0 + own_hi - c0]))
            else:
                add(0, lb, "const", lambda lo, hi: ppc(PP_CHI + h))
                add(hb_, 1024, "const", lambda lo, hi: ppc(PP_CLO + h))
        else:
            kp = ktg - 8
            if kp == 0:
                add(0, 896, "const", lambda lo, hi: ppc(PP_COT + h))
                add(896, 1024, "tile", lambda lo, hi: cn[:, 0, lo - 896:hi - 896])
            elif kp == 7:
                add(0, 128, "tile", lambda lo, hi: cn[:, 1, lo:hi])
                add(128, 1024, "const", lambda lo, hi: ppc(PP_COT + h))
            else:
                add(0, 1024, "const", lambda lo, hi: ppc(PP_COT + h))
        if c0 + w > 1024:
            for jj in range(2):
                segs.append((1024 + jj - c0, 1025 + jj - c0, "const0", hb[:, ktg, jj:jj + 1]))
        return segs

    pairc = {"n": 0, "p": 0, "t": 0}

    def diff_attention(h, g, hooks={}):
        hl = h % 4
        bf = h % 2
        bkey = "bias%d" % bf
        tasks = [(c, ktg) for c in range(3) for ktg in range(16)]
        base = pairc["n"]
        pairc["n"] += len(tasks) + 1
        QB = [(0, 342), (342, 342), (684, 340)]

        def p0_of(i):
            return 2 * ((base + i) % 2)

        def emit_qk(i):
            c, ktg = tasks[i]
            c0, w = QB[c]
            p0 = p0_of(i)

            def qk(eng, _s, p0=p0, ktg=ktg, c0=c0, w=w):
                eng.matmul(ps[:, p0, 0:w], dkT[bf][0:64, ktg * 128:(ktg + 1) * 128], dqT[bf][0:64, c0:c0 + w],
                           start=True, stop=True)
                return eng.matmul(ps[:, p0 + 1, 0:w], dkT[bf][64:128, ktg * 128:(ktg + 1) * 128],
                                  dqT[bf][64:128, c0:c0 + w], start=True, stop=True)
            sc.op("pe", qk, reads=["dk%d_%d" % (bf, ktg // 4), "dq%d_%d" % (bf, c)],
                  writes=["ps%d" % p0, "ps%d" % (p0 + 1)])

        def emit_exp(i):
            c, ktg = tasks[i]
            c0, w = QB[c]
            p0 = p0_of(i)
            pi = (base + i) % 2
            P = Pb[pi]
            pkeys = []
            for si_, (lo, hi, kind, data) in enumerate(diff_segments(h, ktg, c0, w)):
                pk = "P%d_%d" % (pi, si_)
                if kind in ("const", "const0"):
                    act(P[:, :, lo:hi], ps[:, p0:p0 + 2, lo:hi], AF.Exp, ["ps%d" % p0, "ps%d" % (p0 + 1), "pp", bkey],
                        [pk], bias=data, scale=SC_D)
                    pkeys.append(pk)
                else:
                    pkeys.append(pk)
                    ti = pairc["t"] % 2; pairc["t"] += 1
                    tm = tmpb[ti]
                    for m in range(2):
                        dve(lambda e, m=m, tm=tm, lo=lo, hi=hi, data=data, p0=p0: e.scalar_tensor_tensor(
                            tm[:, m, lo:hi], ps[:, p0 + m, lo:hi], SC_D, data, ALU.mult, ALU.add),
                            ["ps%d" % (p0 + m), bkey], ["tmp%d_%d" % (ti, m)])
                    act(P[:, :, lo:hi], tm[:, :, lo:hi], AF.Exp, ["tmp%d_0" % ti, "tmp%d_1" % ti], [pk])
            return P, pkeys

        def emit_pv(i, P, pkeys):
            c, ktg = tasks[i]
            w = QB[c][1]

            def pv(eng, _s, P=P, ktg=ktg, w=w):
                st, sp_ = (ktg == 0), (ktg == 15)
                vv = dv_g[:, ktg, hl * 128:(hl + 1) * 128]
                eng.matmul(ps[:, 4, 0:w], vv, P[:, 0, 0:w], start=st, stop=sp_)
                eng.matmul(ps[:, 5, 0:w], vv, P[:, 1, 0:w], start=st, stop=sp_)
                eng.matmul(ps[:, 6, 0:w], ones_b, P[:, 0, 0:w], start=st, stop=sp_)
                return eng.matmul(ps[:, 7, 0:w], ones_b, P[:, 1, 0:w], start=st, stop=sp_)
            sc.op("pe", pv, reads=pkeys + ["dv%d" % ktg, "ones_b"], writes=["ps4", "ps5", "ps6", "ps7"])

        def fin_head(c):
            w = QB[c][1]
            dve(lambda e: e.tensor_copy(R1[:, 0:w], ps[:, 4, 0:w]), ["ps4"], ["R1"])
            dve(lambda e: e.tensor_copy(R2[:, 0:w], ps[:, 5, 0:w]), ["ps5"], ["R2"])
            act(RR[:, :, 0:w], ps[:, 6:8, 0:w], AF.Ln, ["ps6", "ps7"], ["RR"])

        def fin_mid(c):
            c0, w = QB[c]
            act(RR[:, :, 0:w], RR[:, :, 0:w], AF.Exp, ["RR"], ["RR"], scale=-1.0)
            dve(lambda e: e.tensor_tensor(R1[:, 0:w], R1[:, 0:w], RR[:, 0, 0:w], ALU.mult), ["R1", "RR"], ["R1"])
            dve(lambda e: e.tensor_tensor(R2[:, 0:w], R2[:, 0:w], RR[:, 1, 0:w], ALU.mult), ["R2", "RR"], ["R2"])
            dve(lambda e: e.scalar_tensor_tensor(A_[:, 0:w], R2[:, 0:w], neglam, R1[:, 0:w], ALU.mult, ALU.add),
                ["R1", "R2", "neglam"], ["A_"])
            dve(lambda e: e.tensor_tensor(SQd[:, 0:w], A_[:, 0:w], A_[:, 0:w], ALU.mult), ["A_"], ["SQd"])

        def fin_pe(c, p0):
            c0, w = QB[c]
            mm_one(ps[:, p0, 0:w], ones_avg, SQd[:, 0:w], True, True, ["SQd", "ones_avg"], "ps%d" % p0)
            act(RSd[:, 0:w], ps[:, p0, 0:w], AF.Ln, ["ps%d" % p0, "epsc"], ["RSd"], bias=epsc)
            act(RSd[:, 0:w], RSd[:, 0:w], AF.Exp, ["RSd"], ["RSd"], scale=-0.5)
            dve(lambda e: e.scalar_tensor_tensor(aT[:, h, c0:c0 + w], A_[:, 0:w], gcol, RSd[:, 0:w],
                                                 ALU.mult, ALU.mult), ["A_", "RSd", "gcol"], ["aT"])

        emit_qk(0)
        pend_mid = None
        pend_pe = None
        for i in range(len(tasks)):
            if i + 1 < len(tasks):
                emit_qk(i + 1)
            P, pkeys = emit_exp(i)
            emit_pv(i, P, pkeys)
            if pend_mid is not None:
                fin_mid(pend_mid)
                pend_pe = (pend_mid, i + 4)
                pend_mid = None
            if pend_pe is not None and i >= pend_pe[1]:
                fin_pe(pend_pe[0], p0_of(i))
                pend_pe = None
            if tasks[i][1] == 15:
                fin_head(tasks[i][0])
                pend_mid = tasks[i][0]
            if i in hooks:
                hooks[i](p0_of(i))
        fin_mid(pend_mid)
        fin_pe(pend_mid, p0_of(len(tasks) - 1))
        p0 = p0_of(len(tasks))
        tz_, cn_, hb_ = biasb[bf]

        def qkh(eng, _s):
            inst = None
            for ktg in range(16):
                for m in range(2):
                    inst = eng.matmul(ps[:, p0, ktg * 4 + m * 2:ktg * 4 + m * 2 + 2],
                                      dkT[bf][64 * m:64 * m + 64, ktg * 128:(ktg + 1) * 128],
                                      dqT[bf][64 * m:64 * m + 64, 1024:1026], start=True, stop=True)
            return inst
        sc.op("pe", qkh, reads=["dk%d_%d" % (bf, a) for a in range(4)] + ["dq%d_2" % bf], writes=["ps%d" % p0])
        tmh = tmpb[0][:, 0, 0:64]
        Phh = Pb[0][:, 0, 0:64]
        for m in range(2):
            dve(lambda e, m=m: e.scalar_tensor_tensor(
                tmh.rearrange("p (k m j) -> p k m j", k=16, m=2)[:, :, m, :],
                ps[:, p0, 0:64].rearrange("p (k m j) -> p k m j", k=16, m=2)[:, :, m, :], SC_D, hb_,
                ALU.mult, ALU.add), ["ps%d" % p0, bkey], ["tmp0_%d" % m])
        act(Phh, tmh, AF.Exp, ["tmp0_0", "tmp0_1"], ["P0_0"])
        Ph4 = Phh.rearrange("p (k m j) -> p k m j", k=16, m=2)

        def pvh(eng, _s):
            inst = None
            for gi_ in range(4):
                m = gi_ % 2
                for ktg in range(16):
                    lhs = dv_g[:, ktg, hl * 128:(hl + 1) * 128] if gi_ < 2 else ones_b
                    inst = eng.matmul(ps[:, p0 + 1, gi_ * 2:gi_ * 2 + 2], lhs, Ph4[:, ktg, m, :],
                                      start=(ktg == 0), stop=(ktg == 15))
            return inst
        sc.op("pe", pvh, reads=["P0_0", "ones_b"] + ["dv%d" % k for k in range(16)], writes=["ps%d" % (p0 + 1)])
        pa = ps[:, p0 + 1, :]
        pk_ = "ps%d" % (p0 + 1)
        dve(lambda e: e.reciprocal(R1[:, 0:2], pa[:, 4:6]), [pk_], ["R1"])
        dve(lambda e: e.reciprocal(R2[:, 0:2], pa[:, 6:8]), [pk_], ["R2"])
        dve(lambda e: e.tensor_tensor(R1[:, 0:2], pa[:, 0:2], R1[:, 0:2], ALU.mult), [pk_, "R1"], ["R1"])
        dve(lambda e: e.tensor_tensor(R2[:, 0:2], pa[:, 2:4], R2[:, 0:2], ALU.mult), [pk_, "R2"], ["R2"])
        dve(lambda e: e.scalar_tensor_tensor(A_[:, 0:2], R2[:, 0:2], neglam, R1[:, 0:2], ALU.mult, ALU.add),
            ["R1", "R2", "neglam"], ["A_"])
        dve(lambda e: e.tensor_tensor(SQd[:, 0:2], A_[:, 0:2], A_[:, 0:2], ALU.mult), ["A_"], ["SQd"])
        mm_one(ps[:, p0, 0:2], ones_avg, SQd[:, 0:2], True, True, ["SQd", "ones_avg"], "ps%d" % p0)
        act(RSd[:, 0:2], ps[:, p0, 0:2], AF.Ln, ["ps%d" % p0, "epsc"], ["RSd"], bias=epsc)
        act(RSd[:, 0:2], RSd[:, 0:2], AF.Exp, ["RSd"], ["RSd"], scale=-0.5)
        dve(lambda e: e.scalar_tensor_tensor(aT[:, h, 1024:1026], A_[:, 0:2], gcol, RSd[:, 0:2],
                                             ALU.mult, ALU.mult), ["A_", "RSd", "gcol"], ["aT"])

    def diff_project_pieces(h, si):
        bf = h % 2
        sl = wslot(si)
        pieces = []
        for c in range(4):
            def pc(b=None, c=c):
                if b is None:
                    b = psbank()
                rhs = hT_chunk512(c)
                mm_group(ps[:, b, 0:512], [(sl[:, kc, 128:256], rhs[:, kc, :]) for kc in range(KC)],
                         HK_OWN + HK_OTH + ["w%d" % si], "ps%d" % b)
                evac(dkT[bf][:, c * 512:(c + 1) * 512], ps[:, b, 0:512], ["ps%d" % b], ["dk%d_%d" % (bf, c)])
            pieces.append(pc)
        for c in range(3):
            def pq(b=None, c=c):
                if b is None:
                    b = psbank()
                c0 = QCH[c][0]
                mm_group(ps[:, b, 0:342], [(sl[:, kc, 0:128], hT_own[:, kc, c0:c0 + 342]) for kc in range(KC)],
                         HK_OWN + ["w%d" % si], "ps%d" % b)
                evac(dqT[bf][:, c0:c0 + 342], ps[:, b, 0:342], ["ps%d" % b], ["dq%d_%d" % (bf, c)])
            pieces.append(pq)
        return pieces

    rr["psmod"] = 4
    nxt = diff_load(0)
    pend_pieces = diff_project_pieces(0, nxt)
    for g in range(2):
        sv0 = load_w([(w_in[:, 2048 + g * 512:2048 + g * 512 + 256], 0)])
        sv1 = load_w([(w_in[:, 2048 + g * 512 + 256:2048 + g * 512 + 512], 0)])
        for kt in range(16):
            b = psbank()
            ht = hT_tile(kt)

            def fn(eng, _s, b=b, ht=ht, sv0=sv0, sv1=sv1):
                inst = None
                for half, sv in ((0, sv0), (1, sv1)):
                    sl = wslot(sv)
                    for kc in range(KC):
                        inst = eng.matmul(ps[:, b, half * 256:(half + 1) * 256], ht[:, kc, :], sl[:, kc, 0:256],
                                          start=(kc == 0), stop=(kc == KC - 1))
                return inst
            sc.op("pe", fn, reads=HKT(kt) + ["w%d" % sv0, "w%d" % sv1], writes=["ps%d" % b])
            evac(dv_g[:, kt, :], ps[:, b, :], ["ps%d" % b], ["dv%d" % kt])
        for hl in range(4):
            h = 4 * g + hl
            for pcs_ in pend_pieces:
                pcs_()
            pend_pieces = []
            if stop == 'B1':
                return finish_dbg()
            hooks = {}
            if h < 7:
                nxt = diff_load(h + 1)
                pend_pieces = diff_project_pieces(h + 1, nxt)
            try:
                diff_attention(h, g, hooks)
            except _Cut:
                return finish_dbg()
            if stop == 'B2':
                return finish_dbg()
    sc.barrier()
    if stop == 'B':
        return finish_dbg()

    wqT_g = bfv(28728, 4 * NQ).rearrange("p (c n) -> p c n", c=4)
    wkT_g = bfv(30780, 2048)
    wv_g = bfv(31804, 2048).rearrange("p (c n) -> p c n", c=16)
    tzw_g = f32v(32828, 1536).rearrange("p (a n) -> p a n", a=3)
    tze_g = f32v(34364, 1024).rearrange("p (a n) -> p a n", a=2)
    hbw_g = f32v(35388, 24).rearrange("p (a b c) -> p a b c", a=2, b=3)
    Pw = [bfv(35412, 512), bfv(35668, 512)]
    tmpw = [f32v(35924, 512), f32v(36436, 512)]
    DEN = f32v(36948, 512)
    tmph = f32v(37460, 4)
    DENh = f32v(37464, 4)
    Ph = bfv(37500, 8)[:, 0:4]
    SC_W = 128 ** -0.5
    for g in range(2):
        def fnb(e, sem, g=g):
            e.dma_start(out=tzw_g, in_=tzw_d[:, g, :, :]).then_inc(sem, 16)
            e.dma_start(out=tze_g, in_=tze_d[:, g, :, :]).then_inc(sem, 16)
            i3 = e.dma_start(out=hbw_g, in_=hbw_d[:, g, :, :, :])
            i3.then_inc(sem, 16)
            return i3
        sc.op("sp", fnb, writes=["wbias"], dma=("wbias", 3))
        skv = load_w([(w_in[:, 4096 + g * 128:4096 + (g + 1) * 128], 0),
                      (w_in[:, 4352 + g * 128:4352 + (g + 1) * 128], 128)])
        sq0 = load_w([(w_in[:, 3072 + g * 512:3072 + g * 512 + 256], 0)])
        sq1 = load_w([(w_in[:, 3072 + g * 512 + 256:3072 + g * 512 + 512], 0)])
        proj_fm(skv, 0, hT_chunk512, 4, 512, lambda c: wkT_g[:, c * 512:(c + 1) * 512], HK_OWN + HK_OTH, "wk")
        slkv = wslot(skv)
        for kt in range(16):
            b = psbank()
            ht = hT_tile(kt)
            mm_group(ps[:, b, 0:128], [(ht[:, kc, :], slkv[:, kc, 128:256]) for kc in range(KC)],
                     HKT(kt) + ["w%d" % skv], "ps%d" % b)
            evac(wv_g[:, kt, :], ps[:, b, 0:128], ["ps%d" % b], ["wv%d" % kt])
        for gh in range(4):
            sq = sq0 if gh < 2 else sq1
            proj_fm(sq, (gh % 2) * 128, lambda c: hT_own[:, :, QCH[c][0]:QCH[c][0] + 342], 3, 342,
                    lambda c, gh=gh: wqT_g[:, gh, QCH[c][0]:QCH[c][0] + 342], HK_OWN, "wq%d" % gh)
        WQK = ["wq%d_%d" % (a, c) for a in range(4) for c in range(3)]
        wtasks = []
        for qt in range(8):
            for ti, d in enumerate((-1, 0, 1)):
                if qt == 0 and d == -1:
                    wtasks.append((qt, ti, 15, tze_g[:, 0, :]))
                elif qt == 7 and d == 1:
                    wtasks.append((qt, ti, 8, tze_g[:, 1, :]))
                else:
                    wtasks.append((qt, ti, qt + d, tzw_g[:, d + 1, :]))
        wbanks = [psbank() for _ in wtasks]

        def w_qk(i):
            qt, ti, ktg, btile = wtasks[i]
            b = wbanks[i]
            mm_one(ps[:, b, :].rearrange("p (a n) -> p a n", a=4), wkT_g[:, ktg * 128:(ktg + 1) * 128],
                   wqT_g[:, :, qt * 128:(qt + 1) * 128], True, True, ["wk_%d" % (ktg // 4)] + WQK, "ps%d" % b)
        w_qk(0)
        w_qk(1)
        for i, (qt, ti, ktg, btile) in enumerate(wtasks):
            if i + 2 < len(wtasks):
                w_qk(i + 2)
            b = wbanks[i]
            pi = i % 2
            ob = 4 + (qt % 2)
            sb = 6 + (qt % 2)
            dve(lambda e, b=b, pi=pi, btile=btile: e.scalar_tensor_tensor(
                tmpw[pi], ps[:, b, :], SC_W, btile, ALU.mult, ALU.add), ["ps%d" % b, "wbias"], ["tmpw%d" % pi])
            act(Pw[pi], tmpw[pi], AF.Exp, ["tmpw%d" % pi], ["Pw%d" % pi])

            def pvw(eng, _s, ti=ti, ktg=ktg, pi=pi, ob=ob, sb=sb):
                eng.matmul(ps[:, ob, :], wv_g[:, ktg, :], Pw[pi], start=(ti == 0), stop=(ti == 2))
                return eng.matmul(ps[:, sb, :], ones_b, Pw[pi], start=(ti == 0), stop=(ti == 2))
            sc.op("pe", pvw, reads=["Pw%d" % pi, "wv%d" % ktg, "ones_b"], writes=["ps%d" % ob, "ps%d" % sb])
            if ti == 2:
                for gh in range(4):
                    act(DEN[:, gh * 128:(gh + 1) * 128], ps[:, sb, gh * 128:(gh + 1) * 128], AF.Ln,
                        ["ps%d" % sb, "esink"], ["DEN%d" % gh], bias=esink[:, 4 * g + gh:4 * g + gh + 1])
                act(DEN, DEN, AF.Exp, ["DEN%d" % a for a in range(4)], ["DENr"], scale=-1.0)
                dve(lambda e, ob=ob, qt=qt, g=g: e.tensor_tensor(
                    bT[:, 4 * g:4 * g + 4, qt * 128:(qt + 1) * 128], ps[:, ob, :].rearrange("p (a n) -> p a n", a=4),
                    DEN.rearrange("p (a n) -> p a n", a=4), ALU.mult), ["ps%d" % ob, "DENr"], ["bT"])
        for hi, tl in enumerate(((14, 15, 0), (7, 8, 9))):
            ob, sb = 4 + hi, 6 + hi
            for ti, ktg in enumerate(tl):
                b = psbank()
                mm_one(ps[:, b, 0:4].rearrange("p (a n) -> p a n", a=4), wkT_g[:, ktg * 128:(ktg + 1) * 128],
                       wqT_g[:, :, 1024 + hi:1025 + hi], True, True, ["wk_%d" % (ktg // 4)] + WQK, "ps%d" % b)
                dve(lambda e, b=b, hi=hi, ti=ti: e.scalar_tensor_tensor(
                    tmph, ps[:, b, 0:4], SC_W, hbw_g[:, hi, ti, :], ALU.mult, ALU.add),
                    ["ps%d" % b, "wbias"], ["tmph"])
                act(Ph, tmph, AF.Exp, ["tmph"], ["Ph"])

                def pvh(eng, _s, ti=ti, ktg=ktg, ob=ob, sb=sb):
                    eng.matmul(ps[:, ob, 0:4], wv_g[:, ktg, :], Ph, start=(ti == 0), stop=(ti == 2))
                    return eng.matmul(ps[:, sb, 0:4], ones_b, Ph, start=(ti == 0), stop=(ti == 2))
                sc.op("pe", pvh, reads=["Ph", "wv%d" % ktg, "ones_b"], writes=["ps%d" % ob, "ps%d" % sb])
            dve(lambda e, sb=sb, g=g: e.tensor_tensor(DENh, ps[:, sb, 0:4], esink[:, 4 * g:4 * g + 4], ALU.add),
                ["ps%d" % sb, "esink"], ["DENh"])
            dve(lambda e: e.reciprocal(DENh, DENh), ["DENh"], ["DENh"])
            dve(lambda e, ob=ob, hi=hi, g=g: e.tensor_tensor(
                bT[:, 4 * g:4 * g + 4, 1024 + hi:1025 + hi], ps[:, ob, 0:4].rearrange("p (a n) -> p a n", a=4),
                DENh.rearrange("p (a n) -> p a n", a=4), ALU.mult), ["ps%d" % ob, "DENh"], ["bT"])
    sc.barrier()
    if stop == 'C':
        return finish_dbg()

    memb = bfv(8224, 4096).rearrange("p (a n) -> p a n", a=2)
    memT = bfv(10272, 4096).rearrange("p (c n) -> p c n", c=16)
    mkT = bfv(12320, 2048).rearrange("p (c n) -> p c n", c=8)
    mv = bfv(13344, 2048).rearrange("p (a n) -> p a n", a=2)
    mqT = [bfv(28728, 2 * NQ).rearrange("p (a n) -> p a n", a=2), bfv(29754, 2 * NQ).rearrange("p (a n) -> p a n", a=2)]
    Pm = [bfv(30780, 684).rearrange("p (a n) -> p a n", a=2), bfv(31122, 684).rearrange("p (a n) -> p a n", a=2)]
    RM = f32v(31464, 342)
    SC_M = 256 ** -0.5

    def fnm(e, sem):
        i = e.dma_start(out=memb, in_=mem.rearrange("(a p) n -> p a n", p=128))
        i.then_inc(sem, 16)
        return i
    sc.op("pool", fnm, writes=["memb"], dma=("memb", 1))
    for mt in range(2):
        for g4 in range(4):
            b = psbank()
            pb = ps[:, b, :].bitcast(BF16)
            tr_group([(pb[:, j * 128:(j + 1) * 128], memb[:, mt, (g4 * 4 + j) * 128:(g4 * 4 + j + 1) * 128])
                      for j in range(4)], ident_b, ["memb", "ident_b"], "ps%d" % b)
            evac(memT[:, g4 * 4:g4 * 4 + 4, mt * 128:(mt + 1) * 128],
                 pb[:, 0:512].rearrange("p (a n) -> p a n", a=4), ["ps%d" % b], ["memT%d_%d" % (mt, g4)])
    MEMTK = ["memT%d_%d" % (a, c) for a in range(2) for c in range(4)]
    for sp_ in range(4):
        si = load_w([(w_mkv[:, sp_ * 256:(sp_ + 1) * 256], 0)])
        sl = wslot(si)
        for j in range(2):
            b = psbank()
            mm_group(ps[:, b, 0:256], [(sl[:, kc, j * 128:(j + 1) * 128], memT[:, kc, :]) for kc in range(KC)],
                     MEMTK + ["w%d" % si], "ps%d" % b)
            evac(mkT[:, 2 * sp_ + j, :], ps[:, b, 0:256], ["ps%d" % b], ["mkT%d" % (2 * sp_ + j)])
    for sp_ in range(4):
        si = load_w([(w_mkv[:, 1024 + sp_ * 256:1024 + (sp_ + 1) * 256], 0)])
        sl = wslot(si)
        for mt in range(2):
            b = psbank()
            mm_group(ps[:, b, 0:256], [(memT[:, kc, mt * 128:(mt + 1) * 128], sl[:, kc, 0:256]) for kc in range(KC)],
                     MEMTK + ["w%d" % si], "ps%d" % b)
            evac(mv[:, mt, sp_ * 256:(sp_ + 1) * 256], ps[:, b, 0:256], ["ps%d" % b], ["mv%d_%d" % (mt, sp_)])
    for hm in range(4):
        si = load_w([(w_in[:, 4608 + hm * 256:4608 + (hm + 1) * 256], 0)])
        mq = mqT[hm % 2]
        for dd in range(2):
            proj_fm(si, dd * 128, lambda c: hT_own[:, :, QCH[c][0]:QCH[c][0] + 342], 3, 342,
                    lambda c, dd=dd, mq=mq: mq[:, dd, QCH[c][0]:QCH[c][0] + 342], HK_OWN, "mq%d_%d" % (hm % 2, dd))
        for c, (c0, w) in enumerate(QCH):
            pi = (hm * 3 + c) % 2
            for mt in range(2):
                b = psbank()
                mm_group(ps[:, b, 0:w], [(mkT[:, 2 * hm + dd, mt * 128:(mt + 1) * 128], mq[:, dd, c0:c0 + w])
                                         for dd in range(2)],
                         ["mkT%d" % (2 * hm), "mkT%d" % (2 * hm + 1), "mq%d_0_%d" % (hm % 2, c),
                          "mq%d_1_%d" % (hm % 2, c)], "ps%d" % b)
                act(Pm[pi][:, mt, 0:w], ps[:, b, 0:w], AF.Exp, ["ps%d" % b], ["Pm%d_%d" % (pi, mt)], scale=SC_M)
            PK = ["Pm%d_0" % pi, "Pm%d_1" % pi]
            for dvc in range(2):
                mm_group(ps[:, 4 + dvc, 0:w], [(mv[:, mt, hm * 256 + dvc * 128:hm * 256 + (dvc + 1) * 128],
                                                Pm[pi][:, mt, 0:w]) for mt in range(2)],
                         PK + ["mv%d_%d" % (mt, hm) for mt in range(2)], "ps%d" % (4 + dvc))
            mm_group(ps[:, 6, 0:w], [(ones_b, Pm[pi][:, mt, 0:w]) for mt in range(2)], PK + ["ones_b"], "ps6")
            dve(lambda e, w=w: e.reciprocal(RM[:, 0:w], ps[:, 6, 0:w]), ["ps6"], ["RM"])
            for dvc in range(2):
                dve(lambda e, dvc=dvc, hm=hm, c0=c0, w=w: e.tensor_tensor(
                    cT[:, 2 * hm + dvc, c0:c0 + w], ps[:, 4 + dvc, 0:w], RM[:, 0:w], ALU.mult),
                    ["ps%d" % (4 + dvc), "RM"], ["cT"])
    sc.barrier()
    if stop == 'D':
        return finish_dbg()

    rr["psmod"] = 8
    gT = bfv(32832, 16 * NQ).rearrange("p (c n) -> p c n", c=16)
    SG = [f32v(8224, NQ), f32v(9250, NQ)]
    GAs = [f32v(10276, NQ), f32v(11302, NQ)]
    TP = f32v(12328, NQ)
    brT = [aT, bT, cT]
    BRK = ["aT", "bT", "cT"]

    def v3(ap):
        return ap.rearrange("p (a n) -> p a n", a=3)
    sgi = 0
    for dp in range(8):
        for n in range(3):
            sg_ = load_w([(w_gate[:, n * 2048 + dp * 256:n * 2048 + (dp + 1) * 256], 0)])
            sb_ = load_w([(w_br[n, :, dp * 256:(dp + 1) * 256], 0)])
            slg, slb = wslot(sg_), wslot(sb_)
            for j in range(2):
                dc = 2 * dp + j

                def fng(eng, _s, slg=slg, j=j):
                    inst = None
                    for c, (c0, w) in enumerate(QCH):
                        for kc in range(KC):
                            inst = eng.matmul(ps[:, c, 0:w], slg[:, kc, j * 128:(j + 1) * 128], hT_own[:, kc, c0:c0 + w],
                                              start=(kc == 0), stop=(kc == KC - 1))
                    return inst
                sc.op("pe", fng, reads=HK_OWN + ["w%d" % sg_], writes=["ps0", "ps1", "ps2"])

                def fnw(eng, _s, slb=slb, j=j, n=n):
                    inst = None
                    for c, (c0, w) in enumerate(QCH):
                        for kc in range(8):
                            inst = eng.matmul(ps[:, 3 + c, 0:w], slb[:, kc, j * 128:(j + 1) * 128],
                                              brT[n][:, kc, c0:c0 + w], start=(kc == 0), stop=(kc == 7))
                    return inst
                sc.op("pe", fnw, reads=[BRK[n], "w%d" % sb_], writes=["ps3", "ps4", "ps5"])
                sgb = SG[sgi % 2]; sk = "SG%d" % (sgi % 2); sgi += 1
                act(v3(sgb), ps[:, 0:3, 0:342], AF.Sigmoid, ["ps0", "ps1", "ps2", "pp"], [sk],
                    bias=ppc(PP_BG + n * 16 + dc))
                GA = GAs[j]; gk = "GA%d" % j
                if n == 0:
                    dve(lambda e, sgb=sgb, GA=GA: e.tensor_tensor(v3(GA), v3(sgb), ps[:, 3:6, 0:342], ALU.mult),
                        [sk, "ps3", "ps4", "ps5"], [gk])
                else:
                    dve(lambda e, sgb=sgb: e.tensor_tensor(v3(TP), v3(sgb), ps[:, 3:6, 0:342], ALU.mult),
                        [sk, "ps3", "ps4", "ps5"], ["TP"])
                    if n == 1:
                        dve(lambda e, GA=GA: e.tensor_tensor(GA, GA, TP, ALU.add), [gk, "TP"], [gk])
                    else:
                        dve(lambda e, dc=dc, GA=GA: e.tensor_tensor(gT[:, dc, :], GA, TP, ALU.add), [gk, "TP"],
                            ["gT%d" % dc])
    sc.barrier()
    if stop == 'E':
        return finish_dbg()

    vT = f32v(16416, 16 * NQ).rearrange("p (c n) -> p c n", c=16)
    h1T = hT_own
    xtF = [f32v(8224, 2048), f32v(10272, 2048)]
    GTK = ["gT%d" % dc for dc in range(16)]
    def f_stage1(t):
        sl = t % 2
        xk = "xtF%d" % sl
        if t < 8:
            dma_in(xtF[sl], xs[t * 128:(t + 1) * 128, :], xk, xk)
            st = t
        else:
            dma_in(xtF[sl], xh, xk, xk)
            st = 16
            ln_stats(xtF[sl], 16, xk)
        act(xtF[sl], xtF[sl], AF.Identity, [xk, "sc%d" % st, "scb%d" % st], [xk],
            bias=stat_sc[:, st, 1:2], scale=stat_sc[:, st, 0:1])

    def f_stage2(t):
        sl = t % 2
        xk = "xtF%d" % sl
        for g4 in range(4):
            b = psbank()
            tr_group([(ps[:, b, j * 128:(j + 1) * 128], xtF[sl][:, (g4 * 4 + j) * 128:(g4 * 4 + j + 1) * 128])
                      for j in range(4)], ident_f, [xk, "ident_f"], "ps%d" % b)
            for j in range(4):
                dc = g4 * 4 + j
                if t < 8:
                    dst = vT[:, dc, t * 128:(t + 1) * 128]; src = ps[:, b, j * 128:(j + 1) * 128]
                else:
                    dst = vT[:, dc, 1024:1026]; src = ps[:, b, j * 128:j * 128 + 2]
                if g4 % 2 == 0:
                    act(dst, src, AF.Identity, ["ps%d" % b, "ag0", "ab0"], ["v%d_%d" % (dc, t)],
                        bias=ab0[:, dc:dc + 1], scale=ag0[:, dc:dc + 1])
                else:
                    dve(lambda e, dst=dst, src=src, dc=dc: e.tensor_scalar(
                        dst, src, ag0[:, dc:dc + 1], ab0[:, dc:dc + 1], ALU.mult, ALU.add),
                        ["ps%d" % b, "ag0", "ab0"], ["v%d_%d" % (dc, t)])
    f_stage1(0)
    for t in range(9):
        if t + 1 < 9:
            f_stage1(t + 1)
        f_stage2(t)
    for dp in range(8):
        si = load_w([(w_o[:, dp * 256:(dp + 1) * 256], 0)])
        sl_ = wslot(si)
        for j in range(2):
            dc = 2 * dp + j
            b0 = 0 if (dc % 2 == 0) else 3

            def fno(eng, _s, sl_=sl_, j=j, b0=b0):
                inst = None
                for c, (c0, w) in enumerate(QCH):
                    for kc in range(KC):
                        inst = eng.matmul(ps[:, b0 + c, 0:w], sl_[:, kc, j * 128:(j + 1) * 128], gT[:, kc, c0:c0 + w],
                                          start=(kc == 0), stop=(kc == KC - 1))
                return inst
            pk = ["ps%d" % (b0 + c) for c in range(3)]
            sc.op("pe", fno, reads=GTK + ["w%d" % si], writes=pk)
            dve(lambda e, dc=dc, b0=b0: e.tensor_tensor(v3(vT[:, dc, :]), v3(vT[:, dc, :]), ps[:, b0:b0 + 3, 0:342],
                                                       ALU.add), pk + ["v%d_%d" % (dc, t) for t in range(9)],
                ["v%d" % dc])
    sc.barrier()
    if stop == 'F2':
        return finish_dbg()

    def layer_norm_fm(V, ncol, gcolf, bcolf, post, tmpbase, nchunks, cw):
        S1 = f32v(tmpbase, ncol); S2 = f32v(tmpbase + 1026, ncol)
        SQ = [f32v(tmpbase + 2052, ncol), f32v(tmpbase + 3078, ncol)]

        def vc(ap):
            return ap.rearrange("p (a n) -> p a n", a=nchunks)
        for dc in range(16):
            vv = V[:, dc, 0:ncol]
            if dc == 0:
                dve(lambda e, vv=vv: e.tensor_copy(S1, vv), ["v0"], ["S1"])
            else:
                dve(lambda e, vv=vv: e.tensor_tensor(S1, S1, vv, ALU.add), ["v%d" % dc, "S1"], ["S1"])
            q = SQ[dc % 2]
            act(q, vv, AF.Square, ["v%d" % dc], ["SQ%d" % (dc % 2)])
            if dc == 0:
                dve(lambda e, q=q: e.tensor_copy(S2, q), ["SQ0"], ["S2"])
            else:
                dve(lambda e, q=q: e.tensor_tensor(S2, S2, q, ALU.add), ["SQ%d" % (dc % 2), "S2"], ["S2"])
        for c in range(nchunks):
            mm_one(ps[:, c, 0:cw], ones_f, S1[:, c * cw:(c + 1) * cw], True, True, ["S1", "ones_f"], "ps%d" % c)
            mm_one(ps[:, 4 + c, 0:cw], ones_f, S2[:, c * cw:(c + 1) * cw], True, True, ["S2", "ones_f"],
                   "ps%d" % (4 + c))
        pa = ["ps%d" % c for c in range(nchunks)]
        pb_ = ["ps%d" % (4 + c) for c in range(nchunks)]
        dve(lambda e: e.tensor_scalar(vc(S1), ps[:, 0:nchunks, 0:cw], 1.0 / D, None, ALU.mult), pa, ["S1"])
        dve(lambda e: e.tensor_tensor(SQ[0], S1, S1, ALU.mult), ["S1"], ["SQ0"])
        dve(lambda e: e.scalar_tensor_tensor(vc(S2), ps[:, 4:4 + nchunks, 0:cw], 1.0 / D, vc(SQ[0]), ALU.mult,
                                             ALU.subtract), pb_ + ["SQ0"], ["S2"])
        act(S2, S2, AF.Ln, ["S2", "epsc"], ["S2"], bias=epsc)
        act(S2, S2, AF.Exp, ["S2"], ["S2"], scale=-0.5)
        for dc in range(16):
            vv = V[:, dc, 0:ncol]
            dve(lambda e, vv=vv: e.tensor_tensor(vv, vv, S1, ALU.subtract), ["v%d" % dc, "S1"], ["v%d" % dc])
            dve(lambda e, vv=vv: e.tensor_tensor(vv, vv, S2, ALU.mult), ["v%d" % dc, "S2"], ["v%d" % dc])
            post(dc, vv)

    def post1(dc, vv):
        act(h1T[:, dc, 0:NQ], vv, AF.Identity, ["v%d" % dc, "pp"], ["h1T%d" % dc],
            bias=ppc(PP_LN + 48 + dc), scale=ppc(PP_LN + 32 + dc))
        dve(lambda e: e.tensor_scalar(vv, vv, ag1[:, dc:dc + 1], ab1[:, dc:dc + 1], ALU.mult, ALU.add),
            ["v%d" % dc, "ag1", "ab1"], ["v%d" % dc])
    layer_norm_fm(vT, NQ, None, None, post1, 8224, 3, 342)
    sc.barrier()
    if stop == 'F':
        return finish_dbg()

    H1K = ["h1T%d" % dc for dc in range(16)]
    actT = [bfv(8224, 8192).rearrange("p (c n) -> p c n", c=8), bfv(12320, 8192).rearrange("p (c n) -> p c n", c=8)]
    UV = [f32v(32832, NQ), f32v(33858, NQ)]
    UG = [f32v(34884, NQ), f32v(35910, NQ)]
    CT = [f32v(36936, 1024), f32v(37960, 1024)]
    GL = f32v(38984, 1024)
    groups = [list(range(a, min(a + 8, NJ))) for a in range(0, NJ, 8)]
    prc = 0
    prcc = {"n": 0}

    def ffn_up(gi, grp):
        ab_ = gi % 2
        for jj, j in enumerate(grp):
            si = load_w([(w_up[:, j * 128:(j + 1) * 128], 0), (w_up[:, D_FF + j * 128:D_FF + (j + 1) * 128], 128)])
            sl_ = wslot(si)
            u = (gi * 8 + jj) % 2
            for which, (U, b0, coff, chunk) in enumerate(((UV[u], 0, 0, j), (UG[u], 3, 128, NJ + j))):
                def fnu(eng, _s, sl_=sl_, b0=b0, coff=coff):
                    inst = None
                    for c, (c0, w) in enumerate(QCH):
                        for kc in range(KC):
                            inst = eng.matmul(ps[:, b0 + c, 0:w], sl_[:, kc, coff:coff + 128], h1T[:, kc, c0:c0 + w],
                                              start=(kc == 0), stop=(kc == KC - 1))
                    return inst
                pk = ["ps%d" % (b0 + c) for c in range(3)]
                sc.op("pe", fnu, reads=H1K + ["w%d" % si], writes=pk)
                uk = "U%d_%d" % (which, u)
                act(U[:, 1:685].rearrange("p (a n) -> p a n", a=2), ps[:, b0:b0 + 2, 0:342], AF.Copy, pk[0:2], [uk + "a"])
                act(U[:, 685:1025], ps[:, b0 + 2, 0:340], AF.Copy, [pk[2]], [uk + "b"])
                act(U[:, 0:1], ps[:, b0 + 2, 340:341], AF.Identity, [pk[2], "pp"], [uk + "c"], scale=ppc(PP_FLAG))
                act(U[:, 1025:1026], ps[:, b0 + 2, 341:342], AF.Identity, [pk[2], "pp"], [uk + "d"],
                    scale=ppc(PP_FLAG + 1))
                ukeys = [uk + x for x in "abcd"]
                ct = CT[which]
                ck = "CT%d" % which
                dve(lambda e, U=U, ct=ct, chunk=chunk: e.tensor_scalar(ct, U[:, 0:1024], ppc(PP_CW + chunk), None,
                                                                       ALU.mult), ukeys + ["pp"], [ck])
                for k in (1, 2):
                    dve(lambda e, U=U, ct=ct, chunk=chunk, k=k: e.scalar_tensor_tensor(
                        ct, U[:, k:k + 1024], ppc(PP_CW + 86 * k + chunk), ct, ALU.mult, ALU.add),
                        ukeys + [ck, "pp"], [ck])
            act(GL, CT[1], AF.Gelu_apprx_tanh, ["CT1", "pp"], ["GL"], bias=ppc(PP_CB + NJ + j))
            dve(lambda e, ab_=ab_, jj=jj, j=j: e.scalar_tensor_tensor(
                actT[ab_][:, jj, :], CT[0], ppc(PP_CB + j), GL, ALU.add, ALU.mult), ["CT0", "GL", "pp"],
                ["actT%d_%d" % (ab_, jj)])

    def ffn_down(gi, grp):
        ab_ = gi % 2
        nj = len(grp)
        j0 = grp[0]
        AK = ["actT%d_%d" % (ab_, jj) for jj in range(nj)]
        for dp in range(8):
            si = load_w([(w_down[j0 * 128:(j0 + nj) * 128, dp * 256:(dp + 1) * 256], 0)])
            sl_ = wslot(si)
            for jd in range(2):
                dc = 2 * dp + jd
                b0 = (6, 0, 2, 4)[prcc["n"] % 4]; prcc["n"] += 1

                def fnd(eng, _s, sl_=sl_, jd=jd, b0=b0, nj=nj, ab_=ab_):
                    inst = None
                    for hf in range(2):
                        for jj in range(nj):
                            inst = eng.matmul(ps[:, b0 + hf, :], sl_[:, jj, jd * 128:(jd + 1) * 128],
                                              actT[ab_][:, jj, hf * 512:(hf + 1) * 512], start=(jj == 0),
                                              stop=(jj == nj - 1))
                    return inst
                pk = ["ps%d" % b0, "ps%d" % (b0 + 1)]
                sc.op("pe", fnd, reads=AK + ["w%d" % si], writes=pk)
                dve(lambda e, dc=dc, b0=b0: e.tensor_tensor(
                    vT[:, dc, 0:1024].rearrange("p (a n) -> p a n", a=2),
                    vT[:, dc, 0:1024].rearrange("p (a n) -> p a n", a=2), ps[:, b0:b0 + 2, :], ALU.add),
                    pk + ["v%d" % dc], ["v%d" % dc])

    ffn_up(0, groups[0])
    for gi in range(1, len(groups)):
        ffn_up(gi, groups[gi])
        ffn_down(gi - 1, groups[gi - 1])
    ffn_down(len(groups) - 1, groups[-1])
    sc.barrier()
    if stop == 'G':
        return finish_dbg()

    def post2(dc, vv):
        dve(lambda e: e.tensor_scalar(vv, vv, ppc(PP_LN + 64 + dc), ppc(PP_LN + 80 + dc), ALU.mult, ALU.add),
            ["v%d" % dc, "pp"], ["v%d" % dc])
    layer_norm_fm(vT, 1024, None, None, post2, 32832, 2, 512)
    OUTT = [f32v(36936, 2048), f32v(38984, 2048)]
    VK = ["v%d" % dc for dc in range(16)]
    for t in range(8):
        o = OUTT[t % 2]
        ok = "OUT%d" % (t % 2)
        for g4 in range(4):
            b = psbank()
            tr_group([(ps[:, b, j * 128:(j + 1) * 128], vT[:, g4 * 4 + j, t * 128:(t + 1) * 128]) for j in range(4)],
                     ident_f, VK + ["ident_f"], "ps%d" % b)
            evac(o[:, g4 * 512:(g4 + 1) * 512], ps[:, b, :], ["ps%d" % b], [ok + "_%d" % g4])

        def fny(e, sem, o=o, t=t):
            i = e.dma_start(out=y[t * 128:(t + 1) * 128, :], in_=o)
            i.then_inc(sem, 16)
            return i
        sc.op("sp", fny, reads=[ok + "_%d" % g4 for g4 in range(4)], writes=["y%d" % t], dma=(ok, 1))
    sc.op("sp", lambda e, _s: e.nop(), reads=["y%d" % t for t in range(8)], writes=["done"])
    sc.emit(nc, stack, block)
    stack.close()
    nc._sched = sc
    return nc


def _t5_bucket(rel):
    half, me = 16, 8
    rel = np.asarray(rel, dtype=np.int64)
    ret = np.where(rel > 0, half, 0)
    n = np.abs(rel)
    nf = np.maximum(n, 1).astype(np.float32)
    large = me + (np.log(nf / np.float32(me)) / np.float32(math.log(128 / 8)) * np.float32(half - me)).astype(np.int32)
    large = np.minimum(large, half - 1)
    return ret + np.where(n < me, n, large)


_NC_CACHE = {}


def _prep(inp):
    f = lambda k: np.ascontiguousarray(np.asarray(inp[k], dtype=np.float32))
    x = f("x"); memx = f("mem"); table = f("rel_table")
    w_in = f("w_in")[0]; w_mkv = f("w_mem_kv")[0]; w_gate = f("w_gate")[0]; w_br = f("w_branch")[0]
    w_o = f("w_o")[0]; w_up = f("w_up")[0]; w_down = f("w_down")[0]
    i128 = np.arange(128)

    def cols16(v):
        return np.asarray(v, np.float32).reshape(16, 128).T

    pp0 = np.zeros((128, PP_N), np.float32)
    for n_, k in enumerate(("ln_in_g", "ln_in_b")):
        pp0[:, PP_LN + 16 * n_:PP_LN + 16 * (n_ + 1)] = cols16(f(k))
    for n_, k in enumerate(("ln1_g", "ln1_b", "ln2_g", "ln2_b")):
        pp0[:, PP_LN + 32 + 16 * n_:PP_LN + 32 + 16 * (n_ + 1)] = cols16(f(k)[0])
    pp0[:, PP_BG:PP_BG + 48] = f("b_gate")[0].reshape(48, 128).T
    cw = f("conv_w")[0]
    for k in range(3):
        pp0[:, PP_CW + 86 * k:PP_CW + 86 * (k + 1)] = cw[k].reshape(86, 128).T
    pp0[:, PP_CB:PP_CB + 86] = f("conv_b")[0].reshape(86, 128).T
    pp0[:, PP_SG] = f("diff_subln_g")[0]
    for n_, k in enumerate(("diff_lq1", "diff_lk1", "diff_lq2", "diff_lk2")):
        pp0[:, PP_LQ + 64 * n_:PP_LQ + 64 * (n_ + 1)] = f(k)[0][None, :]
    pp0[:, PP_SINK:PP_SINK + 8] = f("win_sink")[0][None, :]
    pp0[:, PP_CHI:PP_CHI + 8] = table[31, 0:8][None, :]
    pp0[:, PP_CLO:PP_CLO + 8] = table[15, 0:8][None, :]

    r = i128[:, None] - np.arange(384)[None, :] + 128
    tz0 = np.transpose(table[_t5_bucket(r)][:, :, 0:8], (0, 2, 1))
    tzd = np.empty((128, 8, 1068), np.float32)
    tzd[:, :, 0:342] = table[31, 0:8][None, :, None]
    tzd[:, :, 342:726] = tz0
    tzd[:, :, 726:1068] = table[15, 0:8][None, :, None]
    tzw = np.zeros((128, 2, 3, 512), np.float32)
    for d in (-1, 0, 1):
        rel = 128 * d + i128[:, None] - i128[None, :]
        bk = _t5_bucket(rel)
        valid = np.abs(rel) <= 128
        for g in range(2):
            for gh in range(4):
                tzw[:, g, d + 1, gh * 128:(gh + 1) * 128] = np.where(valid, table[bk, 8 + 4 * g + gh], MASKV)
    cm = np.eye(128, dtype=np.float32)

    in_maps = []
    for c in range(8):
        b, half = c // 2, c % 2
        own0, oth0 = half * 1024, (1 - half) * 1024
        xo, xt_ = x[b, own0:own0 + 1024], x[b, oth0:oth0 + 1024]
        xs = np.ascontiguousarray(np.concatenate([xo, xt_], axis=0))
        xh = np.ascontiguousarray(np.concatenate([xt_[1023:1024], xt_[0:127]], axis=0))
        pp = pp0.copy()
        pp[:, PP_COT:PP_COT + 8] = table[31 if half == 0 else 15, 0:8][None, :]
        pp[:, PP_FLAG] = 1.0 if half == 1 else 0.0
        pp[:, PP_FLAG + 1] = 1.0 if half == 0 else 0.0

        def kpos(ktg):
            return (own0 + ktg * 128 if ktg < 8 else oth0 + (ktg - 8) * 128) + i128
        cnd = np.zeros((128, 8, 2, 128), np.float32)
        for e, (ktg, qt) in enumerate(((8, 7), (15, 0))):
            rel = kpos(ktg)[:, None] - (own0 + qt * 128 + i128)[None, :]
            cnd[:, :, e, :] = np.transpose(table[_t5_bucket(rel)][:, :, 0:8], (0, 2, 1))
        hq = (own0 - 1, own0 + 1024)
        hbd = np.zeros((128, 8, 16, 2), np.float32)
        for ktg in range(16):
            for jj in range(2):
                rel = kpos(ktg) - hq[jj]
                hbd[:, :, ktg, jj] = table[_t5_bucket(rel)][:, 0:8]
        tze = np.zeros((128, 2, 2, 512), np.float32)
        for e, (ktg, qt) in enumerate(((15, 0), (8, 7))):
            rel = kpos(ktg)[:, None] - (own0 + qt * 128 + i128)[None, :]
            bk = _t5_bucket(rel)
            valid = np.abs(rel) <= 128
            for g in range(2):
                for gh in range(4):
                    tze[:, g, e, gh * 128:(gh + 1) * 128] = np.where(valid, table[bk, 8 + 4 * g + gh], MASKV)
        hbw = np.zeros((128, 2, 2, 3, 4), np.float32)
        for hi, tl in enumerate(((14, 15, 0), (7, 8, 9))):
            if not (0 <= hq[hi] < 2048):
                continue
            for ti, ktg in enumerate(tl):
                rel = kpos(ktg) - hq[hi]
                bk = _t5_bucket(rel)
                valid = np.abs(rel) <= 128
                for g in range(2):
                    for gh in range(4):
                        hbw[:, g, hi, ti, gh] = np.where(valid, table[bk, 8 + 4 * g + gh], MASKV)
        in_maps.append({
            "xs": xs, "xh": xh, "mem": np.ascontiguousarray(memx[b]), "w_in": w_in, "w_mkv": w_mkv,
            "w_gate": w_gate, "w_br": w_br, "w_o": w_o, "w_up": w_up, "w_down": w_down, "pp": pp,
            "tzd": tzd, "cnd": cnd, "hbd": hbd, "tzw": tzw, "tze": tze, "hbw": hbw, "cm": cm,
        })
    return in_maps


def kernel(**inp):
    in_maps = _prep(inp)
    if "nc" not in _NC_CACHE:
        _NC_CACHE["nc"] = build_program()
    res = run_bass_kernel_spmd(_NC_CACHE["nc"], in_maps, core_ids=list(range(8)))
    out = np.zeros((4, 2048, 2048), np.float32)
    for c in range(8):
        b, half = c // 2, c % 2
        out[b, half * 1024:(half + 1) * 1024] = np.asarray(res.results[c]["y"], dtype=np.float32)
    return out
```
